# Optimizing a Trainium2 kernel written in Bass

```python
import jax, jax.numpy as jnp
from jax import lax
import numpy as np

D_MODEL = 2048
BATCH = 1
SEQ = 8192
DEPTH = 1
DEC_BATCH = 128
DEC_SEQ = 1
PAST_LEN = 8192
PAGE_SIZE = 128

ATT_HEADS = 16
ATT_KV_HEADS = 2
ATT_HEAD_DIM = 64
ATT_GROUP = ATT_HEADS // ATT_KV_HEADS
WINDOW = 128
BLOCK = WINDOW
RET_HEADS = 4
RET_DK = 256
RET_DV = 256
RET_CHUNK = 128
ROPE_BASE = 10000.0
D_FF = 4 * D_MODEL
EPS = 1e-6

ATT_WIDTH = ATT_HEADS * ATT_HEAD_DIM
KV_WIDTH = ATT_KV_HEADS * ATT_HEAD_DIM
RET_QK_WIDTH = RET_HEADS * RET_DK
RET_V_WIDTH = RET_HEADS * RET_DV
MIX_WIDTH = ATT_WIDTH + RET_V_WIDTH
IN_WIDTH = ATT_WIDTH + 2 * KV_WIDTH + 2 * RET_QK_WIDTH + 2 * RET_V_WIDTH
IN_SPLITS = (ATT_WIDTH,
             ATT_WIDTH + KV_WIDTH,
             ATT_WIDTH + 2 * KV_WIDTH,
             ATT_WIDTH + 2 * KV_WIDTH + RET_QK_WIDTH,
             ATT_WIDTH + 2 * KV_WIDTH + 2 * RET_QK_WIDTH,
             ATT_WIDTH + 2 * KV_WIDTH + 2 * RET_QK_WIDTH + RET_V_WIDTH)

kernel_name = 'hymba_swa_sink_retention_decoder_step'


def _rmsnorm(x, g):
    xf = x.astype(jnp.float32)
    var = jnp.mean(jnp.square(xf), axis=-1, keepdims=True)
    return (xf * lax.rsqrt(var + EPS) * g.astype(jnp.float32)).astype(x.dtype)


def _rotary(x, pos):
    half = x.shape[-1] // 2
    inv = ROPE_BASE ** (-jnp.arange(half, dtype=jnp.float32) / half)
    ang = pos.astype(jnp.float32)[:, None] * inv[None, :]
    cos = jnp.cos(ang)[None, :, None, :]
    sin = jnp.sin(ang)[None, :, None, :]
    xf = x.astype(jnp.float32)
    x1, x2 = xf[..., :half], xf[..., half:]
    return jnp.concatenate([x1 * cos - x2 * sin, x2 * cos + x1 * sin], axis=-1).astype(x.dtype)


def _sink_softmax(s, mask, sinks):
    s = jnp.where(mask, s, -jnp.inf)
    sk = sinks.astype(jnp.float32).reshape(ATT_KV_HEADS, ATT_GROUP)[:, :, None, None]
    m = jnp.maximum(jnp.max(s, axis=-1, keepdims=True), sk)
    p = jnp.exp(s - m)
    return p / (jnp.sum(p, axis=-1, keepdims=True) + jnp.exp(sk - m))


def _swa_prompt(q, k, v, sinks):
    B, L = q.shape[:2]
    nb = L // BLOCK
    qb = q.reshape(B, nb, BLOCK, ATT_KV_HEADS, ATT_GROUP, ATT_HEAD_DIM)
    kb = k.reshape(B, nb, BLOCK, ATT_KV_HEADS, ATT_HEAD_DIM)
    vb = v.reshape(B, nb, BLOCK, ATT_KV_HEADS, ATT_HEAD_DIM)
    shift = ((0, 0), (1, 0), (0, 0), (0, 0), (0, 0))
    kk = jnp.concatenate([jnp.pad(kb[:, :-1], shift), kb], axis=2)
    vv = jnp.concatenate([jnp.pad(vb[:, :-1], shift), vb], axis=2)
    s = jnp.einsum('bnikgd,bnjkd->bnkgij', qb, kk,
                   preferred_element_type=jnp.float32) * (ATT_HEAD_DIM ** -0.5)
    blk = jnp.arange(nb)[:, None]
    qpos = blk * BLOCK + jnp.arange(BLOCK)[None, :]
    kpos = (blk - 1) * BLOCK + jnp.arange(2 * BLOCK)[None, :]
    rel = qpos[:, :, None] - kpos[:, None, :]
    mask = (rel >= 0) & (rel <= WINDOW) & (kpos[:, None, :] >= 0)
    p = _sink_softmax(s, mask[None, :, None, None], sinks)
    o = jnp.einsum('bnkgij,bnjkd->bnikgd', p.astype(v.dtype), vv)
    wb = min(WINDOW, L)
    return o.reshape(B, L, ATT_WIDTH), k[:, L - wb:], v[:, L - wb:]


def _swa_decode(q, k, v, k_buf, v_buf, sinks):
    B, T = q.shape[:2]
    W = k_buf.shape[1]
    kk = jnp.concatenate([k_buf, k], axis=1)
    vv = jnp.concatenate([v_buf, v], axis=1)
    qg = q.reshape(B, T, ATT_KV_HEADS, ATT_GROUP, ATT_HEAD_DIM)
    s = jnp.einsum('btkgd,bjkd->bkgtj', qg, kk,
                   preferred_element_type=jnp.float32) * (ATT_HEAD_DIM ** -0.5)
    qpos = PAST_LEN + jnp.arange(T)
    kpos = jnp.concatenate([PAST_LEN - W + jnp.arange(W), PAST_LEN + jnp.arange(T)])
    rel = qpos[:, None] - kpos[None, :]
    mask = (rel >= 0) & (rel <= WINDOW)
    p = _sink_softmax(s, mask[None, None, None], sinks)
    o = jnp.einsum('bkgtj,bjkd->btkgd', p.astype(v.dtype), vv)
    return o.reshape(B, T, ATT_WIDTH), kk[:, T:], vv[:, T:]


def _retention(q, k, v, s0):
    B, L = q.shape[:2]
    C = RET_CHUNK if L % RET_CHUNK == 0 else L
    n = L // C
    log_g = jnp.log1p(-jnp.exp2(-5.0 - jnp.arange(RET_HEADS, dtype=jnp.float32)))
    idx = jnp.arange(C, dtype=jnp.float32)
    rel = idx[:, None] - idx[None, :]
    dmask = jnp.where(rel[None] >= 0,
                      jnp.exp(log_g[:, None, None] * jnp.maximum(rel, 0.0)[None]), 0.0)
    qc = q.reshape(B, n, C, RET_HEADS, RET_DK)
    kc = k.reshape(B, n, C, RET_HEADS, RET_DK)
    vc = v.reshape(B, n, C, RET_HEADS, RET_DV)
    att = jnp.einsum('bnihd,bnjhd->bnhij', qc, kc) * dmask
    o_intra = jnp.einsum('bnhij,bnjhv->bnihv', att, vc)
    k_dec = kc * jnp.exp(log_g[None, :] * (C - 1.0 - idx)[:, None])[None, None, :, :, None]
    kv = jnp.einsum('bnjhd,bnjhv->bnhdv', k_dec, vc)
    chunk_decay = jnp.exp(log_g * C)[:, None, None]

    def step(s, kv_n):
        return chunk_decay * s + kv_n, s

    s_final, s_prev = lax.scan(step, s0, jnp.moveaxis(kv, 1, 0))
    s_prev = jnp.moveaxis(s_prev, 0, 1)
    q_dec = qc * jnp.exp(log_g[None, :] * (idx + 1.0)[:, None])[None, None, :, :, None]
    o_inter = jnp.einsum('bnihd,bnhdv->bnihv', q_dec, s_prev)
    return (o_intra + o_inter).reshape(B, L, RET_HEADS, RET_DV), s_final


def _layer(x, pos, k_buf, v_buf, s0, ln1_g, w_in, q_norm_g, k_norm_g, attn_sinks,
           ret_norm_g, w_out, ln2_g, w_up, w_down):
    B, L, _ = x.shape
    h = _rmsnorm(x, ln1_g)
    z = h @ w_in
    aq, ak, av, rq, rk, rv, rg = jnp.split(z, IN_SPLITS, axis=-1)
    aq = _rmsnorm(aq.reshape(B, L, ATT_HEADS, ATT_HEAD_DIM), q_norm_g)
    ak = _rmsnorm(ak.reshape(B, L, ATT_KV_HEADS, ATT_HEAD_DIM), k_norm_g)
    av = av.reshape(B, L, ATT_KV_HEADS, ATT_HEAD_DIM)
    if k_buf is None:
        a_out, k_new, v_new = _swa_prompt(aq, ak, av, attn_sinks)
    else:
        a_out, k_new, v_new = _swa_decode(aq, ak, av, k_buf, v_buf, attn_sinks)
    rq = _rotary(rq.reshape(B, L, RET_HEADS, RET_DK), pos)
    rk = _rotary(rk.reshape(B, L, RET_HEADS, RET_DK), pos)
    rv = rv.reshape(B, L, RET_HEADS, RET_DV)
    r_o, s_new = _retention(rq.astype(jnp.float32),
                            rk.astype(jnp.float32) * (RET_DK ** -0.5),
                            rv.astype(jnp.float32), s0.astype(jnp.float32))
    r_o = _rmsnorm(r_o, ret_norm_g.reshape(RET_HEADS, RET_DV)).reshape(B, L, RET_V_WIDTH)
    r_out = r_o.astype(x.dtype) * jax.nn.silu(rg)
    x = x + jnp.concatenate([a_out, r_out], axis=-1) @ w_out
    h2 = _rmsnorm(x, ln2_g)
    x = x + jnp.square(jax.nn.relu(h2 @ w_up)) @ w_down
    return x, k_new, v_new, s_new.astype(x.dtype)


def setup_inputs(seed: int = 0) -> dict:
    key = jax.random.key(seed)
    ks = jax.random.split(key, 16)
    wb = min(WINDOW, PAST_LEN)
    f32 = jnp.float32
    nrm = lambda k, shape, scale: jax.random.normal(k, shape, f32) * scale
    return {
        'x_prompt': nrm(ks[0], (BATCH, SEQ, D_MODEL), 1.0),
        'x_sample': nrm(ks[1], (DEC_BATCH, DEC_SEQ, D_MODEL), 1.0),
        'cache_k_win': nrm(ks[2], (DEPTH, DEC_BATCH, wb, ATT_KV_HEADS, ATT_HEAD_DIM), 1.0),
        'cache_v_win': nrm(ks[3], (DEPTH, DEC_BATCH, wb, ATT_KV_HEADS, ATT_HEAD_DIM), 1.0),
        'state_ret': nrm(ks[4], (DEPTH, DEC_BATCH, RET_HEADS, RET_DK, RET_DV), 0.3),
        'ln1_g': 1.0 + nrm(ks[5], (DEPTH, D_MODEL), 0.02),
        'w_in': nrm(ks[6], (DEPTH, D_MODEL, IN_WIDTH), D_MODEL ** -0.5),
        'q_norm_g': 1.0 + nrm(ks[7], (DEPTH, ATT_HEAD_DIM), 0.02),
        'k_norm_g': 1.0 + nrm(ks[8], (DEPTH, ATT_HEAD_DIM), 0.02),
        'attn_sinks': nrm(ks[9], (DEPTH, ATT_HEADS), 0.5),
        'ret_norm_g': 1.0 + nrm(ks[10], (DEPTH, RET_V_WIDTH), 0.02),
        'w_out': nrm(ks[11], (DEPTH, MIX_WIDTH, D_MODEL), MIX_WIDTH ** -0.5),
        'ln2_g': 1.0 + nrm(ks[12], (DEPTH, D_MODEL), 0.02),
        'w_up': nrm(ks[13], (DEPTH, D_MODEL, D_FF), D_MODEL ** -0.5),
        'w_down': nrm(ks[14], (DEPTH, D_FF, D_MODEL), D_FF ** -0.5),
    }


def reference(x_prompt, x_sample, cache_k_win, cache_v_win, state_ret, ln1_g, w_in, q_norm_g,
              k_norm_g, attn_sinks, ret_norm_g, w_out, ln2_g, w_up, w_down):
    pos_p = jnp.arange(x_prompt.shape[1], dtype=jnp.int32)
    pos_s = PAST_LEN + jnp.arange(x_sample.shape[1], dtype=jnp.int32)
    yp, ys = x_prompt, x_sample
    kp_l, vp_l, sp_l, ks_l, vs_l, ss_l = [], [], [], [], [], []
    s0_p = jnp.zeros((x_prompt.shape[0], RET_HEADS, RET_DK, RET_DV), jnp.float32)
    for l in range(DEPTH):
        w = (ln1_g[l], w_in[l], q_norm_g[l], k_norm_g[l], attn_sinks[l], ret_norm_g[l],
             w_out[l], ln2_g[l], w_up[l], w_down[l])
        yp, kp, vp, sp = _layer(yp, pos_p, None, None, s0_p, *w)
        ys, kn, vn, sn = _layer(ys, pos_s, cache_k_win[l], cache_v_win[l], state_ret[l], *w)
        kp_l.append(kp); vp_l.append(vp); sp_l.append(sp)
        ks_l.append(kn); vs_l.append(vn); ss_l.append(sn)
    return (yp, ys, jnp.stack(kp_l), jnp.stack(vp_l), jnp.stack(sp_l),
            jnp.stack(ks_l), jnp.stack(vs_l), jnp.stack(ss_l))
```

```python
import math
import numpy as np
import ml_dtypes
import concourse.bass as bass
import concourse.mybir as mybir
from concourse.bass_utils import run_bass_kernel_spmd

F32 = mybir.dt.float32
BF16 = mybir.dt.bfloat16
ALU = mybir.AluOpType
AF = mybir.ActivationFunctionType
AX = mybir.AxisListType

EPS = 1e-6
ATT_HEADS, ATT_KV, HD = 16, 2, 64
RH, DK, DV = 4, 256, 256
PAST_LEN = 8192
ROPE_BASE = 10000.0
N_CORES = 8

FULL_CFG = dict(KD=16, NB=8, NS=16, PB=36, GP=4, NPRE=(8, 12, 20, 36), FG=8, NCORES=8)


class Buf:
    __slots__ = ("name", "w", "r", "lsem", "ssem")

    def __init__(self, name):
        self.name = name
        self.w = None
        self.r = {}
        self.lsem = None
        self.ssem = None


class DSem:
    __slots__ = ("h", "count")

    def __init__(self, h):
        self.h = h
        self.count = 0


ENGS = ("pe", "act", "dve", "pool", "sp")


class Sched:
    def __init__(self, nc, psems, dsem_pool):
        self.nc = nc
        self.psem = psems
        self.prog = {e: [] for e in ENGS}
        self.cnt = {e: 0 for e in ENGS}
        self.seen = {e: {} for e in ENGS}
        self.dpool = dsem_pool
        self.dnext = 0
        self.finals = []
        self.all_dma = {}

    def new_dsem(self):
        h = self.dpool(self.dnext)
        self.dnext += 1
        return DSem(h)

    def _waits(self, eng, reads, writes):
        need = {}

        def add(tok):
            sem, val, seng = tok
            if seng == "pe" and eng == "pe":
                return
            k = sem.num
            if k not in need or need[k][1] < val:
                need[k] = (sem, val)

        for b in reads:
            if b.w is not None:
                add(b.w)
        for b in writes:
            if b.w is not None:
                add(b.w)
            for t in b.r.values():
                add(t)
        out = []
        seen = self.seen[eng]
        for k, (sem, val) in need.items():
            if seen.get(k, 0) < val:
                seen[k] = val
                out.append((sem, val))
        return out

    def _mark(self, tok, reads, writes):
        k = tok[0].num
        for b in reads:
            old = b.r.get(k)
            if old is None or old[1] < tok[1]:
                b.r[k] = tok
        for b in writes:
            b.w = tok
            b.r = {}

    def op(self, eng, fn, reads=(), writes=()):
        waits = self._waits(eng, reads, writes)
        self.cnt[eng] += 1
        sem = self.psem[eng]
        tok = (sem, self.cnt[eng], eng)

        def run(e, fn=fn, waits=waits, sem=sem):
            for s, v in waits:
                e.wait_ge(s, v)
            ins = fn(e)
            ins.then_inc(sem, 1)

        self.prog[eng].append(run)
        self._mark(tok, reads, writes)
        self.seen[eng][sem.num] = max(self.seen[eng].get(sem.num, 0), 0)
        return tok

    def dma(self, out, in_, reads=(), writes=(), dsem=None, final=False, q="sp"):
        waits = self._waits(q, reads, writes)
        if dsem is None:
            dsem = self.new_dsem()
        dsem.count += 16
        tok = (dsem.h, dsem.count, "dma")

        def run(e, out=out, in_=in_, waits=waits, h=dsem.h):
            for s, v in waits:
                e.wait_ge(s, v)
            e.dma_start(out=out, in_=in_).then_inc(h, 16)

        self.prog[q].append(run)
        self._mark(tok, reads, writes)
        self.all_dma[dsem.h.num] = (dsem.h, dsem.count)
        if final:
            self.finals.append(tok)
        return tok

    def load(self, buf, out, in_, reads=()):
        if buf.lsem is None:
            buf.lsem = self.new_dsem()
        return self.dma(out, in_, reads=reads, writes=(buf,), dsem=buf.lsem)

    def store(self, buf, out, in_, final=True):
        if buf.ssem is None:
            buf.ssem = self.new_dsem()
        return self.dma(out, in_, reads=(buf,), writes=(), dsem=buf.ssem, final=final)

    def barrier(self):
        targets = [(self.psem[e], self.cnt[e]) for e in ENGS if e != "sp" and self.cnt[e] > 0]
        targets += list(self.all_dma.values())
        for eng in ENGS:
            ws = []
            seen = self.seen[eng]
            for sem, val in targets:
                if sem.num == self.psem[eng].num and eng != "sp":
                    pass
                if seen.get(sem.num, 0) < val:
                    seen[sem.num] = val
                    ws.append((sem, val))
            if ws:
                def run(e, ws=ws):
                    for s, v in ws:
                        e.wait_ge(s, v)
                self.prog[eng].append(run)

    def finish(self):
        ws = [(self.psem[e], self.cnt[e]) for e in ENGS if e != "sp" and self.cnt[e] > 0]
        ws += list(self.all_dma.values())

        def run(e, ws=ws):
            for s, v in ws:
                e.wait_ge(s, v)
        self.prog["sp"].append(run)


class Arena:
    def __init__(self, tensor, nbytes, base=0):
        self.t = tensor
        self.base = base
        self.n = base + nbytes
        self.off = base

    def sub(self, nbytes):
        nb = (nbytes + 31) // 32 * 32
        assert self.off + nb <= self.n, f"arena overflow (sub) {self.off}+{nb}>{self.n}"
        a = Arena(self.t, nb, self.off)
        self.off += nb
        return a

    def rest(self):
        a = Arena(self.t, self.n - self.off, self.off)
        return a

    def clear(self):
        self.off = self.base

    def mark(self):
        return self.off

    def reset(self, m):
        self.off = m

    def alloc(self, shape, dtype):
        esz = 4 if dtype == F32 else 2
        n = 1
        for s in shape[1:]:
            n *= s
        nb = (n * esz + 31) // 32 * 32
        assert self.off + nb <= self.n, f"arena overflow {self.off}+{nb}>{self.n}"
        a = self.t[0:shape[0], self.off // 4:(self.off + nb) // 4]
        self.off += nb
        if dtype != F32:
            a = a.bitcast(dtype)
        a = a[:, 0:n]
        if len(shape) == 3:
            a = a.rearrange("p (a b) -> p a b", a=shape[1])
        elif len(shape) == 4:
            a = a.rearrange("p (a b c) -> p a b c", a=shape[1], b=shape[2])
        elif len(shape) == 5:
            a = a.rearrange("p (a b c d) -> p a b c d", a=shape[1], b=shape[2], c=shape[3])
        return a


def gammas():
    return [1.0 - 2.0 ** (-5.0 - h) for h in range(RH)]


def build_nc(cfg):
    NORM_AHEAD = cfg.get("NORM_AHEAD", True)
    KD, NB, NS, PB, GP, NPRE, FG = (cfg[k] for k in ("KD", "NB", "NS", "PB", "GP", "NPRE", "FG"))
    D = KD * 128
    DFF = FG * 1024
    NT = NB * 128
    NTS = NT + NS
    NKB = NB + 1
    WIN = 1408 + 4096
    gam = gammas()
    cdec = [g ** 128 for g in gam]

    nc = bass.Bass("TRN2", target_bir_lowering=False)

    def din(name, shape, dt=F32):
        return nc.dram_tensor(name, list(shape), dt, kind="ExternalInput").ap()

    def dout(name, shape, dt=F32):
        return nc.dram_tensor(name, list(shape), dt, kind="ExternalOutput").ap()

    xprev = din("xprev", [max(PB, 1) * 128, D])
    xown = din("xown", [NT, D])
    xsmp = din("xsmp", [NS, D])
    ck = din("ck", [NS, 128, 128])
    cv = din("cv", [NS, 128, 128])
    st0 = din("st0", [NS, RH, 256, 256])
    w_in = din("w_in", [D, WIN])
    w_out = din("w_out", [2048, D])
    w_up = din("w_up", [D, DFF])
    w_down = din("w_down", [DFF, D])
    g1 = din("g1", [D])
    g2 = din("g2", [D])
    gq = din("gq", [64])
    gk = din("gk", [64])
    sinks = din("sinks", [16])
    gret = din("gret", [1024])
    cosT = din("cosT", [128, NT])
    sinT = din("sinT", [128, NT])
    cosP = din("cosP", [128, max(PB, 1) * 128])
    sinP = din("sinP", [128, max(PB, 1) * 128])
    cosS = din("cosS", [NS, 128])
    sinS = din("sinS", [NS, 128])
    dmaskT = din("dmaskT", [128, RH, 128])
    kdec = din("kdec", [128, RH])
    qdec = din("qdec", [128, RH, 128])
    kdec4 = din("kdec4", [128, RH, GP])
    masks = din("masks", [128, 2, 4, 128], BF16)
    ident_d = din("ident", [128, 128], BF16)
    eye_d = din("eye", [128, NS, NS])
    eyeT_d = din("eyeT", [NS, NS])
    sel_d = din("sel", [NS, NS, 128], BF16)

    y_o = dout("y", [NT, D])
    ys_o = dout("ys", [NS, D])
    kwin_o = dout("kwin", [128, 128])
    vwin_o = dout("vwin", [128, 128])
    sret_o = dout("sret", [RH, 256, 256])
    ksw_o = dout("ksw", [NS, 128, 128])
    vsw_o = dout("vsw", [NS, 128, 128])
    ssn_o = dout("ssn", [NS, RH, 256, 256])

    ARENA_BYTES = 206 * 1024
    import contextlib
    with contextlib.ExitStack() as es:
        arena_t = es.enter_context(nc.sbuf_tensor("arena", [128, ARENA_BYTES // 4], F32))
        banks = [es.enter_context(nc.psum_tensor(f"bank{i}", [128, 512], F32)) for i in range(8)]
        psems = {e: es.enter_context(nc.semaphore(f"prog_{e}")) for e in ENGS}
        dpool = lambda i: es.enter_context(nc.semaphore(f"dsem{i}"))
        block = es.enter_context(nc.Block())

        S = Sched(nc, psems, dpool)
        P0 = Arena(arena_t, ARENA_BYTES)
        PB_ = [Buf(f"bank{i}") for i in range(8)]

        def bank_bf(i):
            return banks[i][:].bitcast(BF16)

        def const(A, shape, dtype, src, name):
            t = A.alloc(shape, dtype)
            b = Buf(name)
            S.load(b, t, src)
            return t, b

        ident, ident_b = const(P0, [128, 128], BF16, ident_d, "ident")
        gqb, gqb_b = const(P0, [128, 64], F32, gq.partition_broadcast(128), "gqb")
        gkb, gkb_b = const(P0, [128, 64], F32, gk.partition_broadcast(128), "gkb")
        snk, snk_b = const(P0, [128, 16], F32, sinks.partition_broadcast(128), "snk")
        gretb, gretb_b = const(P0, [128, 1024], F32, gret.partition_broadcast(128), "gretb")
        dmk, dmk_b = const(P0, [128, RH, 128], F32, dmaskT, "dmk")
        kdc, kdc_b = const(P0, [128, RH], F32, kdec, "kdc")
        qdc, qdc_b = const(P0, [128, RH, 128], F32, qdec, "qdc")
        eye, eye_b = const(P0, [128, NS, NS], F32, eye_d, "eye")
        eyeT, eyeT_b = const(P0, [NS, NS], F32, eyeT_d, "eyeT")
        cSs, cSs_b = const(P0, [NS, 128], F32, cosS, "cosS")
        sSs, sSs_b = const(P0, [NS, 128], F32, sinS, "sinS")

        mhalf = P0.alloc([128, 16], F32)
        ones_bf = P0.alloc([128, 16], BF16)
        cst_b = Buf("cst")
        S.op("pool", lambda e: e.memset(mhalf, -0.5), writes=(cst_b,))
        S.op("pool", lambda e: e.memset(ones_bf, 1.0), writes=(cst_b,))
        esnk = P0.alloc([128, 16], F32)
        esnk_b = Buf("esnk")
        S.op("act", lambda e: e.activation(out=esnk, in_=snk, func=AF.Exp), reads=(snk_b,), writes=(esnk_b,))

        Sst = P0.alloc([128, RH, 2, 256], F32)
        Sbf = P0.alloc([128, RH, 2, 256], BF16)
        Sst_b = [Buf(f"S{h}") for h in range(RH)]
        Sbf_b = [Buf(f"Sbf{h}") for h in range(RH)]
        for h in range(RH):
            S.op("pool", lambda e, h=h: e.memset(Sst[:, h], 0.0), writes=(Sst_b[h],))
            S.op("pool", lambda e, h=h: e.memset(Sbf[:, h], 0.0), writes=(Sbf_b[h],))

        NSTAT = 48
        stat = P0.alloc([128, NSTAT, 8], F32)
        stat_b = [Buf(f"stat{i}") for i in range(NSTAT)]
        stat_i = [0]

        def new_stat():
            i = stat_i[0] % NSTAT
            stat_i[0] += 1
            return stat[:, i], stat_b[i]

        junk = P0.alloc([128, 256], BF16)
        junk_b = Buf("junk")
        hTh = P0.alloc([128, KD, 128], BF16)
        hTh_b = Buf("hTh")

        NSTG, NWB = 4, 2
        stg = [P0.alloc([128, 1024], F32) for _ in range(NSTG)]
        stg_b = [Buf(f"stg{i}") for i in range(NSTG)]
        wbf = [P0.alloc([128, 4096], BF16) for _ in range(NWB)]
        wbf_b = [Buf(f"wbf{i}") for i in range(NWB)]
        wtiles = []
        wstate = dict(loaded=0, cast=0, used=0, piece=0)
        cast_rot = cfg.get("CAST_ROT", ["act", "dve", "act", "pool"])

        def w_view_rows(w, r0, kk, c0, nn):
            return w[r0:r0 + kk * 128, c0:c0 + nn].rearrange("(k p) n -> p k n", p=128)

        def w_pieces(kk):
            nq = min(NSTG, kk)
            return nq, kk // nq

        def w_issue_load(i):
            view, kk, nn = wtiles[i]
            nq, hk = w_pieces(kk)
            for q in range(nq):
                dst = stg[q][:, 0:hk * nn].rearrange("p (k n) -> p k n", k=hk)
                S.load(stg_b[q], dst, view[:, q * hk:(q + 1) * hk, :])

        def w_issue_cast(i):
            view, kk, nn = wtiles[i]
            nq, hk = w_pieces(kk)
            wb = i % NWB
            for q in range(nq):
                src = stg[q][:, 0:hk * nn]
                dst = wbf[wb][:, q * hk * nn:(q + 1) * hk * nn]
                eng = cast_rot[wstate["piece"] % len(cast_rot)]
                wstate["piece"] += 1
                if eng == "act":
                    S.op("act", lambda e, d=dst, s_=src: e.activation(out=d, in_=s_, func=AF.Copy),
                         reads=(stg_b[q],), writes=(wbf_b[wb],))
                else:
                    S.op(eng, lambda e, d=dst, s_=src: e.tensor_copy(out=d, in_=s_),
                         reads=(stg_b[q],), writes=(wbf_b[wb],))

        def w_next():
            i = wstate["used"]
            wstate["used"] += 1
            while wstate["cast"] < min(len(wtiles), i + 2):
                j = wstate["cast"]
                if wstate["loaded"] <= j:
                    w_issue_load(j)
                    wstate["loaded"] = j + 1
                w_issue_cast(j)
                wstate["cast"] += 1
            view, kk, nn = wtiles[i]
            wb = i % NWB
            return wbf[wb][:, 0:kk * nn].rearrange("p (k n) -> p k n", k=kk), wbf_b[wb]

        IN_A_Q, IN_A_KV, IN_B = 0, 1024, 1408
        pre_passes = [[2, 3], [0, 1]] if PB > 0 else []
        order = []
        for hs_ in pre_passes:
            for h in hs_:
                order.append(("pre_rk", h))
                order.append(("pre_rv", h))
        for t in range(4):
            order.append(("aq", t))
        order.append(("akv", 0))
        for h in range(RH):
            for nm in ("rq", "rk", "rv", "rg"):
                order.append((nm, h))
        for t in range(D // 256):
            order.append(("wout", t))
        for fg in range(FG):
            for t in range(4):
                order.append(("up", fg * 4 + t))
            for t in range(max(1, D // 512)):
                order.append(("down", fg, t))
        DN = min(512, D)
        for it in order:
            nm = it[0]
            if nm in ("pre_rk", "rk"):
                wtiles.append((w_view_rows(w_in, 0, KD, IN_B + it[1] * 1024 + 256, 256), KD, 256))
            elif nm in ("pre_rv", "rv"):
                wtiles.append((w_view_rows(w_in, 0, KD, IN_B + it[1] * 1024 + 512, 256), KD, 256))
            elif nm == "rq":
                wtiles.append((w_view_rows(w_in, 0, KD, IN_B + it[1] * 1024, 256), KD, 256))
            elif nm == "rg":
                wtiles.append((w_view_rows(w_in, 0, KD, IN_B + it[1] * 1024 + 768, 256), KD, 256))
            elif nm == "aq":
                wtiles.append((w_view_rows(w_in, 0, KD, IN_A_Q + it[1] * 256, 256), KD, 256))
            elif nm == "akv":
                wtiles.append((w_view_rows(w_in, 0, KD, IN_A_KV, 256), KD, 256))
            elif nm == "wout":
                wtiles.append((w_view_rows(w_out, 0, 16, it[1] * 256, 256), 16, 256))
            elif nm == "up":
                wtiles.append((w_view_rows(w_up, 0, KD, it[1] * 256, 256), KD, 256))
            elif nm == "down":
                wtiles.append((w_view_rows(w_down, it[1] * 1024, 8, it[2] * DN, DN), 8, DN))

        RR = P0.rest()
        R1_BYTES = 16 * NTS * 2
        R1 = RR.sub(R1_BYTES)
        R2 = RR.sub(R1_BYTES)
        R3 = RR.rest()
        RRall = Arena(arena_t, RR.n - RR.base, RR.base)

        def rstd_from_ssq(ssq_ap, sb, P, n, ncols=1):
            S.op("pool", lambda e: e.tensor_scalar(out=ssq_ap, in0=ssq_ap, scalar1=1.0 / n, scalar2=EPS,
                                                   op0=ALU.mult, op1=ALU.add), reads=(sb,), writes=(sb,))
            S.op("pool", lambda e: e.tensor_tensor(out=ssq_ap, in0=ssq_ap, in1=mhalf[:P, 0:ncols], op=ALU.pow),
                 reads=(sb, cst_b), writes=(sb,))

        class NormScratch:
            def __init__(self, A, gsrc):
                self.xt = [A.alloc([128, D], F32) for _ in range(2)]
                self.xt_b = [Buf("xt0"), Buf("xt1")]
                self.xg = [A.alloc([128, D], BF16) for _ in range(2)]
                self.xg_b = [Buf("xg0"), Buf("xg1")]
                self.gb, self.gb_b = const(A, [128, D], F32, gsrc.partition_broadcast(128), "gb")
                self.i = 0

        def norm_A(NSC, P, src_dram=None, src_sb=None, src_buf=None):
            i = NSC.i % 2
            NSC.i += 1
            xt, xt_b, xg, xg_b = NSC.xt, NSC.xt_b, NSC.xg, NSC.xg_b
            if src_dram is not None:
                S.load(xt_b[i], xt[i][:P], src_dram)
                src, sbuf_ = xt[i][:P], xt_b[i]
            else:
                src, sbuf_ = src_sb, src_buf
            st_, st_bf = new_stat()
            S.op("act", lambda e: e.activation(out=xg[i][:P], in_=src, func=AF.Square, accum_out=st_[:P, 0:1]),
                 reads=(sbuf_,), writes=(xg_b[i], st_bf))
            rstd_from_ssq(st_[:P, 0:1], st_bf, P, D)
            S.op("dve", lambda e: e.scalar_tensor_tensor(out=xg[i][:P], in0=src, scalar=st_[:P, 0:1], in1=NSC.gb[:P],
                                                         op0=ALU.mult, op1=ALU.mult),
                 reads=(sbuf_, st_bf, NSC.gb_b), writes=(xg_b[i],))
            return (NSC, P, i)

        def norm_B(ctx, dst_fn, dst_bufs, tb=(6, 7)):
            NSC, P, i = ctx
            xg, xg_b = NSC.xg, NSC.xg_b
            nbank = (KD + 7) // 8
            for bi in range(nbank):
                k0, k1 = bi * 8, min(KD, bi * 8 + 8)
                bk = tb[bi % 2]
                pv = bank_bf(bk)[:, 0:(k1 - k0) * 128].rearrange("p (k t) -> p k t", k=k1 - k0)

                def tr(e, k0=k0, k1=k1, pv=pv, i=i):
                    ins = None
                    for k in range(k0, k1):
                        ins = e.transpose(out=pv[:, k - k0, 0:P], in_=xg[i][:P, k * 128:(k + 1) * 128],
                                          identity=ident[:P, :P])
                    return ins
                S.op("pe", tr, reads=(xg_b[i], ident_b), writes=(PB_[bk],))
                d = dst_fn(k0, k1)
                if bi % 2 == 0:
                    S.op("act", lambda e, d=d, pv=pv: e.activation(out=d, in_=pv[:, :, 0:P], func=AF.Copy),
                         reads=(PB_[bk],), writes=tuple(dst_bufs))
                else:
                    S.op("dve", lambda e, d=d, pv=pv: e.tensor_copy(out=d, in_=pv[:, :, 0:P]),
                         reads=(PB_[bk],), writes=tuple(dst_bufs))

        def norm_seq(NSC, items):
            ctxs = {}
            if items:
                it = items[0]
                ctxs[0] = norm_A(NSC, it["P"], it.get("src_dram"), it.get("src_sb"), it.get("src_buf"))
            for n, it in enumerate(items):
                if NORM_AHEAD and n + 1 < len(items):
                    nx = items[n + 1]
                    ctxs[n + 1] = norm_A(NSC, nx["P"], nx.get("src_dram"), nx.get("src_sb"), nx.get("src_buf"))
                if n not in ctxs:
                    ctxs[n] = norm_A(NSC, it["P"], it.get("src_dram"), it.get("src_sb"), it.get("src_buf"))
                norm_B(ctxs.pop(n), it["dst_fn"], it["dst_bufs"])
                if it.get("after") is not None:
                    it["after"]()

        def mm_group(bank_i, out_ap, pairs, reads, extra_writes=()):
            def f(e):
                ins = None
                n = len(pairs)
                for j, (l, r) in enumerate(pairs):
                    ins = e.matmul(out_ap, lhsT=l, rhs=r, start=(j == 0), stop=(j == n - 1))
                return ins
            return S.op("pe", f, reads=tuple(reads), writes=(PB_[bank_i],) + tuple(extra_writes))

        def rotary(x1p, x2p, pbufs, cos_ap, sin_ap, tbufs, o1, o2, obufs, rt, rt_b):
            t1, t2, t3, t4 = rt
            rd = tuple(pbufs) + tuple(tbufs)
            S.op("dve", lambda e: e.tensor_tensor(out=t1, in0=x1p, in1=cos_ap, op=ALU.mult), reads=rd, writes=(rt_b[0],))
            S.op("dve", lambda e: e.tensor_tensor(out=t2, in0=x2p, in1=sin_ap, op=ALU.mult), reads=rd, writes=(rt_b[1],))
            S.op("dve", lambda e: e.tensor_tensor(out=t3, in0=x2p, in1=cos_ap, op=ALU.mult), reads=rd, writes=(rt_b[2],))
            S.op("dve", lambda e: e.tensor_tensor(out=t4, in0=x1p, in1=sin_ap, op=ALU.mult), reads=rd, writes=(rt_b[3],))
            S.op("pool", lambda e: e.tensor_tensor(out=o1, in0=t1, in1=t2, op=ALU.subtract),
                 reads=(rt_b[0], rt_b[1]), writes=tuple(obufs))
            S.op("pool", lambda e: e.tensor_tensor(out=o2, in0=t3, in1=t4, op=ALU.add),
                 reads=(rt_b[2], rt_b[3]), writes=tuple(obufs))

        if PB > 0:
            A = RRall
            A.clear()
            GW = GP * 128
            NSC = NormScratch(A, g1)
            hTg = [A.alloc([128, KD, GW], BF16) for _ in range(2)]
            hTg_b = [Buf(f"hTg{i}") for i in range(2)]
            cpt = [A.alloc([128, GW], F32) for _ in range(2)]
            spt = [A.alloc([128, GW], F32) for _ in range(2)]
            cpt_b = [Buf(f"cpt{i}") for i in range(2)]
            rt2 = [[A.alloc([128, GW], F32) for _ in range(4)] for _ in range(2)]
            rt2_b = [[Buf(f"rt{s}_{i}") for i in range(4)] for s in range(2)]
            pw = [A.alloc([128, KD, 256], BF16) for _ in range(4)]
            pw_b = [Buf(f"pw{i}") for i in range(4)]
            kTg2 = [A.alloc([128, 2, GW], BF16) for _ in range(2)]
            kTg2_b = [Buf("kTg0"), Buf("kTg1")]
            ktok_g2 = [A.alloc([128, GP, 256], BF16) for _ in range(2)]
            ktok_g2b = [Buf("ktokg0"), Buf("ktokg1")]
            rv_g2 = [A.alloc([128, GP, 256], BF16) for _ in range(2)]
            rv_g2b = [Buf("rvg0"), Buf("rvg1")]
            kd4, kd4_b = const(A, [128, RH, GP], F32, kdec4, "kd4")
            hg_i = [0]
            ngroups = PB // GP
            gcount = 0
            for pi_, heads in enumerate(pre_passes):
                for j, h in enumerate(heads):
                    for w_ in range(2):
                        wt, wb_ = w_next()
                        eng = "pool" if w_ == 0 else "dve"
                        S.op(eng, lambda e, j=j, w_=w_, wt=wt: e.tensor_copy(out=pw[2 * j + w_], in_=wt),
                             reads=(wb_,), writes=(pw_b[2 * j + w_],))
                g_first = min((PB - NPRE[h]) // GP for h in heads)
                gi_of = {}
                for g in range(g_first, ngroups):
                    gi_of[g] = gcount % 2
                    gcount += 1

                def head_pieces(g, j_h, h):
                    gi = gi_of[g]
                    wrk, wrv = pw[2 * j_h], pw[2 * j_h + 1]
                    wrk_b, wrv_b = pw_b[2 * j_h], pw_b[2 * j_h + 1]
                    sl = hg_i[0] % 2
                    hg_i[0] += 1
                    rt, rt_b = rt2[sl], rt2_b[sl]
                    kTg, kTg_b = kTg2[sl], kTg2_b[sl]
                    ktok_g, ktok_gb = ktok_g2[sl], ktok_g2b[sl]
                    rv_g, rv_gb = rv_g2[sl], rv_g2b[sl]

                    def p_rk():
                        for dc in range(2):
                            mm_group(dc, banks[dc][:, 0:GW],
                                     [(wrk[:, k, dc * 128:(dc + 1) * 128], hTg[gi][:, k, :]) for k in range(KD)],
                                     reads=(wrk_b, hTg_b[gi]))
                        rotary(banks[0][:, 0:GW], banks[1][:, 0:GW], (PB_[0], PB_[1]), cpt[gi], spt[gi], (cpt_b[gi],),
                               kTg[:, 0, :], kTg[:, 1, :], (kTg_b,), rt, rt_b)

                    def p_rv(j2):
                        bk = 3 + (j2 // 2) % 2
                        nj = min(2, GP - j2)
                        for jj in range(nj):
                            j = j2 + jj
                            mm_group(bk, banks[bk][:, jj * 256:(jj + 1) * 256],
                                     [(hTg[gi][:, k, j * 128:(j + 1) * 128], wrv[:, k, :]) for k in range(KD)],
                                     reads=(wrv_b, hTg_b[gi]))
                        S.op("act", lambda e: e.activation(
                            out=rv_g[:, j2:j2 + nj, :], in_=banks[bk][:, 0:nj * 256].rearrange("p (j c) -> p j c", j=nj),
                            func=AF.Copy), reads=(PB_[bk],), writes=(rv_gb,))

                    def p_tail():
                        pv = bank_bf(2)[:, 0:GP * 256].rearrange("p (j c) -> p j c", j=GP)

                        def trk(e):
                            ins = None
                            for j in range(GP):
                                for dc in range(2):
                                    ins = e.transpose(out=pv[:, j, dc * 128:(dc + 1) * 128],
                                                      in_=kTg[:, dc, j * 128:(j + 1) * 128], identity=ident)
                            return ins
                        S.op("pe", trk, reads=(kTg_b, ident_b), writes=(PB_[2],))
                        for j in range(GP):
                            S.op("act", lambda e, j=j: e.activation(
                                out=ktok_g[:, j, :], in_=pv[:, j, :], func=AF.Copy, scale=kd4[:, h, j:j + 1]),
                                reads=(PB_[2], kd4_b), writes=(ktok_gb,))

                        def kvf(e):
                            ins = None
                            for dc in range(2):
                                for j in range(GP):
                                    ins = e.matmul(banks[5][:, dc * 256:(dc + 1) * 256],
                                                   lhsT=ktok_g[:, j, dc * 128:(dc + 1) * 128], rhs=rv_g[:, j, :],
                                                   start=(j == 0), stop=(j == GP - 1))
                            return ins
                        S.op("pe", kvf, reads=(ktok_gb, rv_gb), writes=(PB_[5],))
                        S.op("dve", lambda e: e.scalar_tensor_tensor(
                            out=Sst[:, h].rearrange("p a b -> p (a b)"), in0=Sst[:, h].rearrange("p a b -> p (a b)"),
                            scalar=float(cdec[h] ** GP), in1=banks[5][:, 0:512], op0=ALU.mult, op1=ALU.add),
                            reads=(PB_[5], Sst_b[h]), writes=(Sst_b[h],))
                    ps_ = [p_rk] + [(lambda j2=j2: p_rv(j2)) for j2 in range(0, GP, 2)] + [p_tail]
                    return ps_

                def heads_pieces(g, heads=heads):
                    out = []
                    for j_h, h in enumerate(heads):
                        if g * GP < PB - NPRE[h]:
                            continue
                        out.extend(head_pieces(g, j_h, h))
                    return out

                pend = []

                def run_pieces(frac):
                    n = len(pend)
                    if n == 0:
                        return
                    k = max(1, (n + frac - 1) // frac)
                    for _ in range(min(k, n)):
                        pend.pop(0)()

                items = []
                for g in range(g_first, ngroups):
                    gi = gi_of[g]
                    pb0 = g * GP
                    for j in range(GP):
                        pb = pb0 + j
                        it = dict(P=128, src_dram=xprev[pb * 128:(pb + 1) * 128, :],
                                  dst_fn=(lambda k0, k1, gi=gi, j=j: hTg[gi][:, k0:k1, j * 128:(j + 1) * 128]),
                                  dst_bufs=[hTg_b[gi]], after=None)
                        def aft(g=g, gi=gi, pi_=pi_, j=j):
                            run_pieces(GP - j)
                            if j == GP - 1:
                                while pend:
                                    pend.pop(0)()
                                if g == ngroups - 1 and pi_ == 0:
                                    S.op("pool", lambda e, gi=gi: e.tensor_copy(out=hTh, in_=hTg[gi][:, :, (GP - 1) * 128:GP * 128]),
                                         reads=(hTg_b[gi],), writes=(hTh_b,))
                                pend.extend(heads_pieces(g))
                                if g >= g_first + 1 and g + 1 < ngroups:
                                    load_tables(g + 1)
                        it["after"] = aft
                        items.append(it)

                def load_tables(g):
                    gi = gi_of[g]
                    S.load(cpt_b[gi], cpt[gi], cosP[:, g * GW:(g + 1) * GW])
                    S.load(cpt_b[gi], spt[gi], sinP[:, g * GW:(g + 1) * GW])
                load_tables(g_first)
                if g_first + 1 < ngroups:
                    load_tables(g_first + 1)
                norm_seq(NSC, items)
                while pend:
                    pend.pop(0)()
            for h in range(RH):
                S.op("pool", lambda e, h=h: e.tensor_copy(out=Sbf[:, h], in_=Sst[:, h]), reads=(Sst_b[h],), writes=(Sbf_b[h],))
        else:
            S.op("pool", lambda e: e.memset(hTh, 0.0), writes=(hTh_b,))
        S.barrier()

        R1.clear(); R2.clear(); R3.clear()
        hT = R1.alloc([128, KD, NTS], BF16)
        hT_b = Buf("hT")
        mixT = R2.alloc([128, 16, NTS], BF16)
        mixT_b = Buf("mixT")
        NSC = NormScratch(R3, g1)
        items = [dict(P=128, src_dram=xown[b * 128:(b + 1) * 128, :],
                      dst_fn=(lambda k0, k1, b=b: hT[:, k0:k1, b * 128:(b + 1) * 128]), dst_bufs=[hT_b]) for b in range(NB)]
        items.append(dict(P=NS, src_dram=xsmp[:, :], dst_fn=(lambda k0, k1: hT[:, k0:k1, NT:NTS]), dst_bufs=[hT_b]))
        norm_seq(NSC, items)
        S.barrier()

        def blk_cols(b):
            return (slice(b * 128, (b + 1) * 128), 128) if b < NB else (slice(NT, NTS), NS)

        R3.clear()
        A = R3
        msk, msk_b = const(A, [128, 2, 4, 128], BF16, masks, "msk")
        qtok_s = A.alloc([128, 1024], BF16)
        qtok_sb = Buf("qtoks")
        ks32 = A.alloc([128, 128], F32)
        vs32 = A.alloc([128, 128], F32)
        ksv_b = Buf("ksv")
        a2_mark = A.mark()
        qtk = [A.alloc([128, 256], BF16) for _ in range(4)]
        qtk_b = [Buf(f"qtk{i}") for i in range(4)]
        qT = A.alloc([128, 8, NT], BF16)
        qT_b = Buf("qT")
        Klo = A.alloc([128, 2, NKB * 128], BF16)
        Khi = A.alloc([128, 2, NKB * 128], BF16)
        K_b = Buf("K")
        vtok = A.alloc([128, NKB, 2, 64], BF16)
        vtok_b = Buf("vtok")
        kdup = A.alloc([128, 2, 2, 2, 64], BF16)
        kdup_b = [Buf("kdup0"), Buf("kdup1")]
        kn32 = A.alloc([128, 128], F32)
        v32 = A.alloc([128, 128], F32)
        kv32_b = Buf("kv32")
        NQ = 4
        sq = [A.alloc([128, 256], F32) for _ in range(NQ)]
        sq_b = [Buf(f"sq{i}") for i in range(NQ)]
        tmpn = [A.alloc([128, 256], F32) for _ in range(NQ)]
        tmpn_b = [Buf(f"tmpn{i}") for i in range(NQ)]
        pT = [A.alloc([128, 512], BF16) for _ in range(3)]
        pT_b = [Buf(f"pT{i}") for i in range(3)]
        aout = A.alloc([128, 1024], BF16)
        aout_b = Buf("aout")
        den = A.alloc([128, 16], F32)
        den_b = Buf("den")

        S.op("pool", lambda e: e.memset(Klo, 0.0), writes=(K_b,))
        S.op("pool", lambda e: e.memset(Khi, 0.0), writes=(K_b,))

        pslot = [0]

        def proj_tok(wt, wb_, src_hT, src_b, cols, P):
            s = pslot[0] % 8
            pslot[0] += 1
            bk, half = (0, 1, 3, 4)[s // 2], s % 2
            out_ap = banks[bk][:P, half * 256:(half + 1) * 256]
            kk = wt.shape[1]
            mm_group(bk, out_ap, [(src_hT[:, k, cols], wt[:, k, :]) for k in range(kk)], reads=(wb_, src_b))
            return out_ap, bk

        def headnorm(ps, bk, P, nh, i2):
            st_, st_bf = new_stat()
            S.op("act", lambda e: e.activation(out=sq[i2][:P, 0:nh * 64], in_=ps[:, 0:nh * 64], func=AF.Square),
                 reads=(PB_[bk],), writes=(sq_b[i2],))
            S.op("dve", lambda e: e.tensor_reduce(out=st_[:P, 0:nh], in_=sq[i2][:P, 0:nh * 64].rearrange("p (h d) -> p h d", h=nh),
                                                  axis=AX.X, op=ALU.add), reads=(sq_b[i2],), writes=(st_bf,))
            rstd_from_ssq(st_[:P, 0:nh], st_bf, P, 64, ncols=nh)
            S.op("dve", lambda e: e.tensor_tensor(
                out=tmpn[i2][:P, 0:nh * 64].rearrange("p (h d) -> p h d", h=nh),
                in0=ps[:, 0:nh * 64].rearrange("p (h d) -> p h d", h=nh),
                in1=st_[:P, 0:nh].unsqueeze(2).broadcast_to([P, nh, 64]), op=ALU.mult),
                reads=(PB_[bk], st_bf), writes=(tmpn_b[i2],))
            return tmpn[i2], tmpn_b[i2]

        def skewed(n_items, stage_fns, enable=True):
            k = len(stage_fns)
            if not enable:
                for i in range(n_items):
                    for f in stage_fns:
                        f(i)
                return
            for t in range(n_items + k - 1):
                for j in range(k):
                    i = t - j
                    if 0 <= i < n_items:
                        stage_fns[j](i)

        pitems = [("q", t, b) for t in range(4) for b in range(NB + 1)] + [("kv", 0, kb) for kb in range(NKB + 1)]
        pctx = {}
        wcur = {}

        def pj_s0(i):
            kind, t, b = pitems[i]
            if (kind, t) not in wcur:
                wcur[(kind, t)] = w_next()
            wt, wb_ = wcur[(kind, t)]
            if kind == "q":
                src_, src_b = hT, hT_b
                cols, P = blk_cols(b)
                nh = 4
            else:
                kb = b
                if kb == 0:
                    src_, src_b, cols, P = hTh, hTh_b, slice(0, 128), 128
                else:
                    src_, src_b = hT, hT_b
                    cols, P = blk_cols(kb - 1) if kb <= NB else blk_cols(NB)
                nh = 2
            ps, bk = proj_tok(wt, wb_, src_, src_b, cols, P)
            i2 = i % NQ
            st_, st_bf = new_stat()
            S.op("act", lambda e: e.activation(out=sq[i2][:P, 0:nh * 64], in_=ps[:, 0:nh * 64], func=AF.Square),
                 reads=(PB_[bk],), writes=(sq_b[i2],))
            pctx[i] = dict(ps=ps, bk=bk, P=P, nh=nh, i2=i2, st=st_, st_b=st_bf)

        def pj_s1(i):
            c = pctx[i]
            P, nh, i2, st_, st_bf = c["P"], c["nh"], c["i2"], c["st"], c["st_b"]
            S.op("dve", lambda e: e.tensor_reduce(out=st_[:P, 0:nh], in_=sq[i2][:P, 0:nh * 64].rearrange("p (h d) -> p h d", h=nh),
                                                  axis=AX.X, op=ALU.add), reads=(sq_b[i2],), writes=(st_bf,))
            rstd_from_ssq(st_[:P, 0:nh], st_bf, P, 64, ncols=nh)

        def pj_s2(i):
            kind, t, b = pitems[i]
            c = pctx[i]
            ps, bk, P, nh, i2, st_, st_bf = c["ps"], c["bk"], c["P"], c["nh"], c["i2"], c["st"], c["st_b"]
            tn, tn_b = tmpn[i2], tmpn_b[i2]
            S.op("dve", lambda e: e.tensor_tensor(
                out=tn[:P, 0:nh * 64].rearrange("p (h d) -> p h d", h=nh),
                in0=ps[:, 0:nh * 64].rearrange("p (h d) -> p h d", h=nh),
                in1=st_[:P, 0:nh].unsqueeze(2).broadcast_to([P, nh, 64]), op=ALU.mult),
                reads=(PB_[bk], st_bf), writes=(tn_b,))
            if kind == "q":
                if b < NB:
                    dq, dq_b = qtk[i2], qtk_b[i2]
                    dqa = dq[:P, :]
                else:
                    dq, dq_b = qtok_s, qtok_sb
                    dqa = qtok_s[:P, t * 256:(t + 1) * 256]
                S.op("dve", lambda e: e.tensor_tensor(
                    out=dqa.rearrange("p (h d) -> p h d", h=4),
                    in0=tn[:P, 0:256].rearrange("p (h d) -> p h d", h=4),
                    in1=gqb[:P].unsqueeze(1).broadcast_to([P, 4, 64]), op=ALU.mult),
                    reads=(tn_b, gqb_b), writes=(dq_b,))
                c["dq"], c["dq_b"] = dq, dq_b
            else:
                kb = b
                if kb <= NB:
                    ks_ = kb % 2
                    S.op("dve", lambda e: e.tensor_tensor(
                        out=kdup[:, ks_], in0=tn[:, 0:128].rearrange("p (g d) -> p g d", g=2).unsqueeze(2).broadcast_to([128, 2, 2, 64]),
                        in1=gkb[:].unsqueeze(1).unsqueeze(1).broadcast_to([128, 2, 2, 64]), op=ALU.mult),
                        reads=(tn_b, gkb_b), writes=(kdup_b[ks_],))
                    S.op("act", lambda e: e.activation(
                        out=vtok[:, kb], in_=ps[:, 128:256].rearrange("p (g d) -> p g d", g=2), func=AF.Copy),
                        reads=(PB_[bk],), writes=(vtok_b,))
                    if kb == NB:
                        S.op("dve", lambda e: e.tensor_tensor(
                            out=kn32[:].rearrange("p (g d) -> p g d", g=2), in0=tn[:, 0:128].rearrange("p (g d) -> p g d", g=2),
                            in1=gkb[:].unsqueeze(1).broadcast_to([128, 2, 64]), op=ALU.mult),
                            reads=(tn_b, gkb_b), writes=(kv32_b,))
                        S.op("act", lambda e: e.activation(out=v32, in_=ps[:, 128:256], func=AF.Copy),
                             reads=(PB_[bk],), writes=(kv32_b,))
                        S.store(kv32_b, kwin_o, kn32[:])
                        S.store(kv32_b, vwin_o, v32[:])
                else:
                    S.op("dve", lambda e: e.tensor_tensor(
                        out=ks32[:NS].rearrange("p (g d) -> p g d", g=2), in0=tn[:NS, 0:128].rearrange("p (g d) -> p g d", g=2),
                        in1=gkb[:NS].unsqueeze(1).broadcast_to([NS, 2, 64]), op=ALU.mult),
                        reads=(tn_b, gkb_b), writes=(ksv_b,))
                    S.op("act", lambda e: e.activation(out=vs32[:NS], in_=ps[:, 128:256], func=AF.Copy),
                         reads=(PB_[bk],), writes=(ksv_b,))

        def pj_s3(i):
            kind, t, b = pitems[i]
            c = pctx.pop(i)
            if kind == "q":
                if b < NB:
                    dq, dq_b = c["dq"], c["dq_b"]
                    pv = bank_bf(2)[:, (i % 2) * 256:(i % 2) * 256 + 256].rearrange("p (c t) -> p c t", c=2)

                    def trq(e):
                        ins = None
                        for cc in range(2):
                            ins = e.transpose(out=pv[:, cc, :], in_=dq[:, cc * 128:(cc + 1) * 128], identity=ident)
                        return ins
                    S.op("pe", trq, reads=(dq_b, ident_b), writes=(PB_[2],))
                    S.op("act", lambda e: e.activation(out=qT[:, 2 * t:2 * t + 2, b * 128:(b + 1) * 128], in_=pv,
                                                       func=AF.Copy), reads=(PB_[2],), writes=(qT_b,))
            else:
                kb = b
                if kb <= NB:
                    ks_ = kb % 2
                    pv = bank_bf(2)[:, 512 + ks_ * 256:512 + ks_ * 256 + 256].rearrange("p (g t) -> p g t", g=2)

                    def trd(e):
                        ins = None
                        for g_ in range(2):
                            ins = e.transpose(out=pv[:, g_, :], in_=kdup[:, ks_, g_].rearrange("p a d -> p (a d)"), identity=ident)
                        return ins
                    S.op("pe", trd, reads=(kdup_b[ks_], ident_b), writes=(PB_[2],))
                    S.op("dve", lambda e: e.tensor_copy(out=Klo[0:64, :, kb * 128:(kb + 1) * 128], in_=pv[0:64]),
                         reads=(PB_[2],), writes=(K_b,))
                    S.op("dve", lambda e: e.tensor_copy(out=Khi[64:128, :, kb * 128:(kb + 1) * 128], in_=pv[64:128]),
                         reads=(PB_[2],), writes=(K_b,))
        if cfg.get("PJ_SKEW2", True):
            def pj_s12(i):
                pj_s1(i)
                pj_s2(i)
            if cfg.get("PJ_SKEW3", True):
                skewed(len(pitems), [pj_s0, pj_s12, pj_s3], True)
            else:
                def pj_s012(i):
                    pj_s0(i)
                    pj_s12(i)
                skewed(len(pitems), [pj_s012, pj_s3], True)
        else:
            skewed(len(pitems), [pj_s0, pj_s1, pj_s2, pj_s3], cfg.get("PJ_SKEW", False))

        aitems = [(n, c) for n in range(NB) for c in range(8)]

        def at_s0(i):
            n, c = aitems[i]
            own = slice((n + 1) * 128, (n + 2) * 128)
            prv = slice(n * 128, (n + 1) * 128)
            qc = slice(n * 128, (n + 1) * 128)
            mk = msk[:, 0 if n == 0 else 1].rearrange("p a t -> p (a t)")
            g_ = c // 4
            sb = 3 + i % 2
            pi = i % 3

            def scf(e):
                ins = None
                for j, (Kt, ks) in enumerate(((Klo, own), (Klo, prv), (Khi, own), (Khi, prv))):
                    ins = e.matmul(banks[sb][:, j * 128:(j + 1) * 128], lhsT=Kt[:, g_, ks], rhs=qT[:, c, qc],
                                   start=True, stop=True)
                return ins
            S.op("pe", scf, reads=(K_b, qT_b), writes=(PB_[sb],))
            S.op("act", lambda e: e.activation(out=pT[pi], in_=banks[sb][:, :], func=AF.Exp, scale=0.125),
                 reads=(PB_[sb],), writes=(pT_b[pi],))
            S.op("dve", lambda e: e.tensor_tensor(out=pT[pi], in0=pT[pi], in1=mk, op=ALU.mult),
                 reads=(pT_b[pi], msk_b), writes=(pT_b[pi],))

        def at_s1(i):
            n, c = aitems[i]
            g_ = c // 4
            pi = i % 3
            qc = slice(n * 128, (n + 1) * 128)

            def pvf(e):
                ins = None
                for par in range(2):
                    h = 2 * c + par
                    ob = 5 + h // 8
                    oo = banks[ob][:, (h % 8) * 64:(h % 8 + 1) * 64]
                    e.matmul(oo, lhsT=pT[pi][:, (2 * par) * 128:(2 * par + 1) * 128], rhs=vtok[:, n + 1, g_, :],
                             start=True, stop=False)
                    e.matmul(oo, lhsT=pT[pi][:, (2 * par + 1) * 128:(2 * par + 2) * 128], rhs=vtok[:, n, g_, :],
                             start=False, stop=True)
                    rr = banks[7][:, h:h + 1]
                    e.matmul(rr, lhsT=pT[pi][:, (2 * par) * 128:(2 * par + 1) * 128], rhs=ones_bf[:, 0:1],
                             start=True, stop=False)
                    ins = e.matmul(rr, lhsT=pT[pi][:, (2 * par + 1) * 128:(2 * par + 2) * 128], rhs=ones_bf[:, 0:1],
                                   start=False, stop=True)
                return ins
            S.op("pe", pvf, reads=(pT_b[pi], vtok_b, cst_b), writes=(PB_[5], PB_[6], PB_[7]))
            if c == 7:
                S.op("dve", lambda e: e.tensor_tensor(out=den, in0=banks[7][:, 0:16], in1=esnk, op=ALU.add),
                     reads=(PB_[7], esnk_b), writes=(den_b,))
                S.op("dve", lambda e: e.reciprocal(out=den, in_=den), reads=(den_b,), writes=(den_b,))
                for hb in range(2):
                    S.op("dve", lambda e, hb=hb: e.tensor_tensor(
                        out=aout[:, hb * 512:(hb + 1) * 512].rearrange("p (h d) -> p h d", h=8),
                        in0=banks[5 + hb][:, :].rearrange("p (h d) -> p h d", h=8),
                        in1=den[:, hb * 8:(hb + 1) * 8].unsqueeze(2).broadcast_to([128, 8, 64]), op=ALU.mult),
                        reads=(PB_[5 + hb], den_b), writes=(aout_b,))
                pv = bank_bf(2)[:, 0:1024].rearrange("p (c t) -> p c t", c=8)

                def tra(e):
                    ins = None
                    for cc in range(8):
                        ins = e.transpose(out=pv[:, cc, :], in_=aout[:, cc * 128:(cc + 1) * 128], identity=ident)
                    return ins
                S.op("pe", tra, reads=(aout_b, ident_b), writes=(PB_[2],))
                S.op("act", lambda e: e.activation(out=mixT[:, 0:8, qc], in_=pv, func=AF.Copy),
                     reads=(PB_[2],), writes=(mixT_b,))
        skewed(len(aitems), [at_s0, at_s1], cfg.get("AT_SKEW", True))
        S.barrier()

        A.reset(a2_mark)
        sel, sel_b = const(A, [NS, NS, 128], BF16, sel_d, "sel")
        aout2 = A.alloc([128, 1024], BF16)
        aout2_b = Buf("aout2")
        den2 = A.alloc([128, 16], F32)
        den2_b = Buf("den2")
        ck_sb = A.alloc([128, NS, 128], F32)
        cv_sb = A.alloc([128, NS, 128], F32)
        cv_bf = A.alloc([128, NS, 128], BF16)
        ckv_b = Buf("ckv")
        cvbf_b = Buf("cvbf")
        S.load(ckv_b, ck_sb, ck.rearrange("b j f -> j b f"))
        S.load(ckv_b, cv_sb, cv.rearrange("b j f -> j b f"))
        S.op("pool", lambda e: e.tensor_copy(out=cv_bf, in_=cv_sb), reads=(ckv_b,), writes=(cvbf_b,))
        S.dma(ksw_o[:, 0:127, :], ck[:, 1:128, :], final=True)
        S.dma(vsw_o[:, 0:127, :], cv[:, 1:128, :], final=True)
        S.store(ksv_b, ksw_o[:, 127, :], ks32[:NS])
        S.store(ksv_b, vsw_o[:, 127, :], vs32[:NS])

        prod = A.alloc([128, 1024], F32)
        prod_b = Buf("prod")
        sTs = A.alloc([128, NS, 16], F32)
        sTs_b = Buf("sTs")
        pTs = A.alloc([128, NS, 16], BF16)
        pTs_b = Buf("pTs")
        Pm = A.alloc([128, 16, NS, NS], BF16)
        Pm_b = Buf("Pm")
        for b in range(NS):
            for hb in range(2):
                mm_group(3 + hb, banks[3 + hb][:, :], [(sel[:NS, b, :], qtok_s[:NS, hb * 512:(hb + 1) * 512])],
                         reads=(sel_b, qtok_sb))
                S.op("dve", lambda e, b=b, hb=hb: e.tensor_tensor(
                    out=prod[:, hb * 512:(hb + 1) * 512].rearrange("p (h d) -> p h d", h=8),
                    in0=banks[3 + hb][:, :].rearrange("p (h d) -> p h d", h=8),
                    in1=ck_sb[:, b, hb * 64:(hb + 1) * 64].unsqueeze(1).broadcast_to([128, 8, 64]), op=ALU.mult),
                    reads=(PB_[3 + hb], ckv_b), writes=(prod_b,))
            S.op("dve", lambda e, b=b: e.tensor_reduce(out=sTs[:, b, :], in_=prod[:].rearrange("p (h d) -> p h d", h=16),
                                                       axis=AX.X, op=ALU.add), reads=(prod_b,), writes=(sTs_b,))
        S.op("act", lambda e: e.activation(out=pTs, in_=sTs, func=AF.Exp, scale=0.125), reads=(sTs_b,), writes=(pTs_b,))
        for h in range(16):
            S.op("dve", lambda e, h=h: e.tensor_tensor(
                out=Pm[:, h], in0=pTs[:, :, h:h + 1].broadcast_to([128, NS, NS]), in1=eye, op=ALU.mult),
                reads=(pTs_b, eye_b), writes=(Pm_b,))

        def spv(e):
            ins = None
            for h in range(16):
                g_ = h // 8
                ob = 5 + h // 8
                for b in range(NS):
                    e.matmul(banks[ob][:NS, (h % 8) * 64:(h % 8 + 1) * 64], lhsT=Pm[:, h, b, :],
                             rhs=cv_bf[:, b, g_ * 64:(g_ + 1) * 64], start=(b == 0), stop=(b == NS - 1))
                for b in range(NS):
                    ins = e.matmul(banks[7][:NS, h:h + 1], lhsT=Pm[:, h, b, :], rhs=ones_bf[:, 0:1],
                                   start=(b == 0), stop=(b == NS - 1))
            return ins
        S.op("pe", spv, reads=(Pm_b, cvbf_b, cst_b), writes=(PB_[5], PB_[6], PB_[7]))
        snew = A.alloc([128, 16], F32)
        pnew = A.alloc([128, 16], F32)
        snew_b = Buf("snew")
        num = A.alloc([128, 1024], F32)
        num_b = Buf("num")
        S.op("dve", lambda e: e.tensor_tensor(
            out=prod[:NS].rearrange("p (g h d) -> p g h d", g=2, h=8),
            in0=qtok_s[:NS, :].rearrange("p (g h d) -> p g h d", g=2, h=8),
            in1=ks32[:NS].rearrange("p (g d) -> p g d", g=2).unsqueeze(2).broadcast_to([NS, 2, 8, 64]), op=ALU.mult),
            reads=(qtok_sb, ksv_b), writes=(prod_b,))
        S.op("dve", lambda e: e.tensor_reduce(out=snew[:NS], in_=prod[:NS].rearrange("p (h d) -> p h d", h=16),
                                              axis=AX.X, op=ALU.add), reads=(prod_b,), writes=(snew_b,))
        S.op("act", lambda e: e.activation(out=pnew[:NS], in_=snew[:NS], func=AF.Exp, scale=0.125),
             reads=(snew_b,), writes=(snew_b,))
        S.op("dve", lambda e: e.tensor_tensor(
            out=num[:NS].rearrange("p (g h d) -> p g h d", g=2, h=8),
            in0=vs32[:NS].rearrange("p (g d) -> p g d", g=2).unsqueeze(2).broadcast_to([NS, 2, 8, 64]),
            in1=pnew[:NS].rearrange("p (g h) -> p g h", g=2).unsqueeze(3).broadcast_to([NS, 2, 8, 64]), op=ALU.mult),
            reads=(ksv_b, snew_b), writes=(num_b,))
        for hb in range(2):
            S.op("dve", lambda e, hb=hb: e.tensor_tensor(out=num[:NS, hb * 512:(hb + 1) * 512], in0=banks[5 + hb][:NS, :],
                                                         in1=num[:NS, hb * 512:(hb + 1) * 512], op=ALU.add),
                 reads=(PB_[5 + hb], num_b), writes=(num_b,))
        S.op("dve", lambda e: e.tensor_tensor(out=den2[:NS], in0=banks[7][:NS, 0:16], in1=esnk[:NS], op=ALU.add),
             reads=(PB_[7], esnk_b), writes=(den2_b,))
        S.op("dve", lambda e: e.tensor_tensor(out=den2[:NS], in0=den2[:NS], in1=pnew[:NS], op=ALU.add),
             reads=(den2_b, snew_b), writes=(den2_b,))
        S.op("dve", lambda e: e.reciprocal(out=den2[:NS], in_=den2[:NS]), reads=(den2_b,), writes=(den2_b,))
        S.op("dve", lambda e: e.tensor_tensor(
            out=aout2[:NS].rearrange("p (h d) -> p h d", h=16), in0=num[:NS].rearrange("p (h d) -> p h d", h=16),
            in1=den2[:NS].unsqueeze(2).broadcast_to([NS, 16, 64]), op=ALU.mult), reads=(num_b, den2_b), writes=(aout2_b,))
        pvs = bank_bf(2)[:, 0:8 * NS].rearrange("p (c t) -> p c t", c=8)

        def tras(e):
            ins = None
            for c in range(8):
                ins = e.transpose(out=pvs[:, c, :], in_=aout2[:NS, c * 128:(c + 1) * 128], identity=ident[:NS, :NS])
            return ins
        S.op("pe", tras, reads=(aout2_b, ident_b), writes=(PB_[2],))
        S.op("act", lambda e: e.activation(out=mixT[:, 0:8, NT:NTS], in_=pvs, func=AF.Copy), reads=(PB_[2],), writes=(mixT_b,))
        S.barrier()

        R3.clear()
        A = R3
        cT = A.alloc([128, NT], F32)
        sT_ = A.alloc([128, NT], F32)
        cs_b = Buf("cs")
        S.load(cs_b, cT, cosT)
        S.load(cs_b, sT_, sinT)
        rt = [A.alloc([128, 512], F32) for _ in range(4)]
        rt_b = [Buf(f"rtB{i}") for i in range(4)]
        qTh = A.alloc([128, 2, NT], BF16)
        kTh = A.alloc([128, 2, NT], BF16)
        qdT = A.alloc([128, 2, NT], BF16)
        qTh_b, kTh_b, qdT_b = Buf("qTh"), Buf("kTh"), Buf("qdT")
        ktok = A.alloc([128, NB, 256], BF16)
        ktok_b = Buf("ktok")
        rv = A.alloc([128, NB, 256], BF16)
        rv_b = Buf("rv")
        gg = A.alloc([128, NB + 1, 256], BF16)
        gg_b = Buf("gg")
        gate32 = [A.alloc([128, 256], F32) for _ in range(2)]
        gate32_b = [Buf("gate0"), Buf("gate1")]
        attm = [A.alloc([128, 128], BF16) for _ in range(2)]
        attm_b = [Buf("attm0"), Buf("attm1")]
        rout = [A.alloc([128, 256], BF16) for _ in range(3)]
        rout_b = [Buf("rout0"), Buf("rout1"), Buf("rout2")]
        SbfB = A.alloc([128, RH, 2, 256], BF16)
        SbfB_b = [Buf(f"SbfB{h}") for h in range(RH)]
        qs32 = A.alloc([128, 256], F32)
        ks32r = A.alloc([128, 256], F32)
        qs_rot = A.alloc([128, 256], F32)
        ks_rot = A.alloc([128, 256], F32)
        qs_bf = A.alloc([128, 256], BF16)
        rvs32 = A.alloc([128, 256], F32)
        rvs_bf = A.alloc([128, 256], BF16)
        smp_b = Buf("smp")
        rtS = [A.alloc([128, 128], F32) for _ in range(4)]
        rtS_b = [Buf(f"rtS{i}") for i in range(4)]
        qsT = A.alloc([128, 2, NS], BF16)
        qsT_b = Buf("qsT")
        Qm = A.alloc([128, 2, NS, NS], BF16)
        Qm_b = Buf("Qm")
        Km = [A.alloc([128, 256], BF16) for _ in range(2)]
        Km_b = [Buf("Km0"), Buf("Km1")]
        s32_b = Buf("s32")
        qThB = A.alloc([128, 2, NT], BF16)
        kThB = A.alloc([128, 2, NT], BF16)
        qdTB = A.alloc([128, 2, NT], BF16)
        qThB_b, kThB_b, qdTB_b = Buf("qThB"), Buf("kThB"), Buf("qdTB")
        NSL = 3
        stt = [A.alloc([128, 2, 256], F32) for _ in range(NSL)]
        stt_b = [Buf(f"stt{i}") for i in range(NSL)]
        sbf = [A.alloc([128, 2, 256], BF16) for _ in range(2)]
        sbf_b = [Buf("sbf0"), Buf("sbf1")]
        qk = A.alloc([128, 8], F32)
        qk_b = Buf("qk")
        os32 = A.alloc([128, 256], F32)
        os_b = Buf("os")
        stt_i = [0]
        r_i = [0]

        def rot_tok(x32, xb, out):
            x1, x2 = x32[:NS, 0:128], x32[:NS, 128:256]
            t1, t2, t3, t4 = [r[:NS] for r in rtS]
            S.op("dve", lambda e: e.tensor_tensor(out=t1, in0=x1, in1=cSs, op=ALU.mult), reads=(xb, cSs_b), writes=(rtS_b[0],))
            S.op("dve", lambda e: e.tensor_tensor(out=t2, in0=x2, in1=sSs, op=ALU.mult), reads=(xb, sSs_b), writes=(rtS_b[1],))
            S.op("dve", lambda e: e.tensor_tensor(out=t3, in0=x2, in1=cSs, op=ALU.mult), reads=(xb, cSs_b), writes=(rtS_b[2],))
            S.op("dve", lambda e: e.tensor_tensor(out=t4, in0=x1, in1=sSs, op=ALU.mult), reads=(xb, sSs_b), writes=(rtS_b[3],))
            S.op("dve", lambda e: e.tensor_tensor(out=out[:NS, 0:128], in0=t1, in1=t2, op=ALU.subtract),
                 reads=(rtS_b[0], rtS_b[1]), writes=(smp_b,))
            S.op("dve", lambda e: e.tensor_tensor(out=out[:NS, 128:256], in0=t3, in1=t4, op=ALU.add),
                 reads=(rtS_b[2], rtS_b[3]), writes=(smp_b,))

        def rng_A(o_ps, o_bk, P, ggrow):
            ri = r_i[0] % 3
            r_i[0] += 1
            st_, st_bf = new_stat()
            S.op("act", lambda e: e.activation(out=junk[:P, 0:256], in_=o_ps, func=AF.Square, accum_out=st_[:P, 0:1]),
                 reads=(PB_[o_bk],), writes=(junk_b, st_bf))
            rstd_from_ssq(st_[:P, 0:1], st_bf, P, 256)
            S.op("dve", lambda e: e.scalar_tensor_tensor(out=rout[ri][:P], in0=o_ps, scalar=st_[:P, 0:1], in1=ggrow,
                                                         op0=ALU.mult, op1=ALU.mult),
                 reads=(PB_[o_bk], st_bf, gg_b), writes=(rout_b[ri],))
            return ri

        def ret_norm_gate(o_ps, o_bk, P, ggrow, cols, h):
            rng_B(rng_A(o_ps, o_bk, P, ggrow), P, cols, h)

        def rng_B(ri, P, cols, h):
            pv = bank_bf(3)[:, 0:2 * P].rearrange("p (c t) -> p c t", c=2)

            def trr(e, ri=ri, pv=pv):
                ins = None
                for c in range(2):
                    ins = e.transpose(out=pv[:, c, :], in_=rout[ri][:P, c * 128:(c + 1) * 128], identity=ident[:P, :P])
                return ins
            S.op("pe", trr, reads=(rout_b[ri], ident_b), writes=(PB_[3],))
            S.op("act", lambda e, pv=pv: e.activation(out=mixT[:, 8 + 2 * h:10 + 2 * h, cols], in_=pv, func=AF.Copy),
                 reads=(PB_[3],), writes=(mixT_b,))

        tgroups = [(t0, min(512, NT - t0)) for t0 in range(0, NT, 512)]
        qTh2, kTh2, qdT2 = [qTh, qThB], [kTh, kThB], [qdT, qdTB]
        qTh2_b, kTh2_b, qdT2_b = [qTh_b, qThB_b], [kTh_b, kThB_b], [qdT_b, qdTB_b]

        def qk_chunks(h, par):
            chunks = []
            wts = {}
            for wi, (dstT, dst_b, s32) in enumerate(((qTh2[par], qTh2_b[par], qs32), (kTh2[par], kTh2_b[par], ks32r))):
                for ti, (t0, tn_) in enumerate(tgroups):
                    def ch(wi=wi, dstT=dstT, dst_b=dst_b, s32=s32, ti=ti, t0=t0, tn_=tn_):
                        if ti == 0:
                            wts[wi] = w_next()
                        wt, wb_ = wts[wi]
                        for dc in range(2):
                            mm_group(dc, banks[dc][:, 0:tn_],
                                     [(wt[:, k, dc * 128:(dc + 1) * 128], hT[:, k, t0:t0 + tn_]) for k in range(KD)],
                                     reads=(wb_, hT_b))
                        def post():
                            rotary(banks[0][:, 0:tn_], banks[1][:, 0:tn_], (PB_[0], PB_[1]), cT[:, t0:t0 + tn_], sT_[:, t0:t0 + tn_],
                                   (cs_b,), dstT[:, 0, t0:t0 + tn_], dstT[:, 1, t0:t0 + tn_], (dst_b,), [r[:, 0:tn_] for r in rt], rt_b)
                            if ti == len(tgroups) - 1:
                                mm_group(2, banks[2][:NS, 0:256], [(hT[:, k, NT:NTS], wt[:, k, :]) for k in range(KD)], reads=(wb_, hT_b))
                                S.op("act", lambda e: e.activation(out=s32[:NS], in_=banks[2][:NS, 0:256], func=AF.Copy,
                                                                   scale=(1.0 if wi == 0 else 1.0 / 16.0)),
                                     reads=(PB_[2],), writes=(s32_b,))
                                if wi == 0:
                                    S.op("pool", lambda e: e.tensor_tensor(
                                        out=qdT2[par][:].rearrange("p c (n i) -> p c n i", i=128),
                                        in0=qTh2[par][:].rearrange("p c (n i) -> p c n i", i=128),
                                        in1=qdc[:, h, :].unsqueeze(1).unsqueeze(1).broadcast_to([128, 2, NB, 128]), op=ALU.mult),
                                        reads=(qTh2_b[par], qdc_b), writes=(qdT2_b[par],))
                        return post
                    chunks.append(ch)
            return chunks

        def head_body(h, par, next_chunks):
            qTh, kTh, qdT = qTh2[par], kTh2[par], qdT2[par]
            qTh_b, kTh_b, qdT_b = qTh2_b[par], kTh2_b[par], qdT2_b[par]
            rot_tok(qs32, s32_b, qs_rot)
            rot_tok(ks32r, s32_b, ks_rot)
            S.op("pool", lambda e: e.tensor_copy(out=qs_bf[:NS], in_=qs_rot[:NS]), reads=(smp_b,), writes=(smp_b,))
            for n0 in range(0, NB, 4):
                nn_ = min(4, NB - n0)
                pv = bank_bf(3)[:, 0:nn_ * 256].rearrange("p (j c) -> p j c", j=nn_)

                def trk2(e, n0=n0, nn_=nn_, pv=pv):
                    ins = None
                    for j in range(nn_):
                        for dc in range(2):
                            ins = e.transpose(out=pv[:, j, dc * 128:(dc + 1) * 128],
                                              in_=kTh[:, dc, (n0 + j) * 128:(n0 + j + 1) * 128], identity=ident)
                    return ins
                S.op("pe", trk2, reads=(kTh_b, ident_b), writes=(PB_[3],))
                S.op("act", lambda e, pv=pv, n0=n0, nn_=nn_, h=h: e.activation(out=ktok[:, n0:n0 + nn_, :], in_=pv, func=AF.Copy,
                                                                                scale=kdc[:, h:h + 1]),
                     reads=(PB_[3], kdc_b), writes=(ktok_b,))
            wt, wb_ = w_next()
            for b in range(NB + 1):
                cols, P = blk_cols(b)
                half = b % 2
                mm_group(2, banks[2][:P, half * 256:(half + 1) * 256], [(hT[:, k, cols], wt[:, k, :]) for k in range(KD)],
                         reads=(wb_, hT_b))
                if b < NB:
                    S.op("act", lambda e, b=b, half=half: e.activation(out=rv[:, b, :], in_=banks[2][:, half * 256:(half + 1) * 256],
                                                                       func=AF.Copy), reads=(PB_[2],), writes=(rv_b,))
                else:
                    S.op("act", lambda e, half=half: e.activation(out=rvs32[:NS], in_=banks[2][:NS, half * 256:(half + 1) * 256],
                                                                  func=AF.Copy), reads=(PB_[2],), writes=(smp_b,))
                    S.op("pool", lambda e: e.tensor_copy(out=rvs_bf[:NS], in_=rvs32[:NS]), reads=(smp_b,), writes=(smp_b,))
            wt, wb_ = w_next()
            for b in range(NB + 1):
                cols, P = blk_cols(b)
                half = b % 2
                gi = b % 2
                mm_group(2, banks[2][:P, half * 256:(half + 1) * 256], [(hT[:, k, cols], wt[:, k, :]) for k in range(KD)],
                         reads=(wb_, hT_b))
                S.op("act", lambda e, P=P, half=half, gi=gi: e.activation(out=gate32[gi][:P], in_=banks[2][:P, half * 256:(half + 1) * 256],
                                                                          func=AF.Silu), reads=(PB_[2],), writes=(gate32_b[gi],))
                S.op("pool", lambda e, P=P, b=b, gi=gi, h=h: e.tensor_tensor(out=gg[:P, b, :], in0=gate32[gi][:P],
                                                                             in1=gretb[:P, h * 256:(h + 1) * 256], op=ALU.mult),
                     reads=(gate32_b[gi], gretb_b), writes=(gg_b,))
            pvq = bank_bf(3)[:, 0:2 * NS].rearrange("p (c t) -> p c t", c=2)

            def trqs(e, pvq=pvq):
                ins = None
                for c in range(2):
                    ins = e.transpose(out=pvq[:, c, :], in_=qs_bf[:NS, c * 128:(c + 1) * 128], identity=ident[:NS, :NS])
                return ins
            S.op("pe", trqs, reads=(smp_b, ident_b), writes=(PB_[3],))
            S.op("act", lambda e, pvq=pvq: e.activation(out=qsT, in_=pvq, func=AF.Copy), reads=(PB_[3],), writes=(qsT_b,))
            for dc in range(2):
                S.op("dve", lambda e, dc=dc: e.tensor_tensor(
                    out=Qm[:, dc], in0=qsT[:, dc, :].unsqueeze(2).broadcast_to([128, NS, NS]), in1=eye, op=ALU.mult),
                    reads=(qsT_b, eye_b), writes=(Qm_b,))
            S.op("dve", lambda e: e.tensor_tensor(out=os32[:NS], in0=qs_rot[:NS], in1=ks_rot[:NS], op=ALU.mult),
                 reads=(smp_b,), writes=(os_b,))
            S.op("dve", lambda e: e.tensor_reduce(out=qk[:NS, 0:1], in_=os32[:NS], axis=AX.X, op=ALU.add),
                 reads=(os_b,), writes=(qk_b,))
            def smp_A(b, h=h):
                si = stt_i[0] % NSL
                bi = stt_i[0] % 2
                stt_i[0] += 1
                S.load(stt_b[si], stt[si], st0[b, h].rearrange("(c p) v -> p c v", p=128))
                S.op("act", lambda e: e.activation(out=sbf[bi], in_=stt[si], func=AF.Copy), reads=(stt_b[si],), writes=(sbf_b[bi],))
                S.op("dve", lambda e: e.tensor_scalar(out=Km[bi][:NS], in0=ks_rot[:NS], scalar1=eyeT[:NS, b:b + 1],
                                                      scalar2=None, op0=ALU.mult),
                     reads=(smp_b, eyeT_b), writes=(Km_b[bi],))
                return (si, bi)

            def smp_B(b, ctx, h=h):
                si, bi = ctx

                def qs0(e):
                    ins = None
                    for dc in range(2):
                        ins = e.matmul(banks[7][:NS, 0:256], lhsT=Qm[:, dc, b, :], rhs=sbf[bi][:, dc, :],
                                       start=(b == 0 and dc == 0), stop=(b == NS - 1 and dc == 1))
                    return ins
                S.op("pe", qs0, reads=(Qm_b, sbf_b[bi]), writes=(PB_[7],))

                def kvs(e):
                    ins = None
                    for dc in range(2):
                        ins = e.matmul(banks[2][:, dc * 256:(dc + 1) * 256], lhsT=Km[bi][:NS, dc * 128:(dc + 1) * 128],
                                       rhs=rvs_bf[:NS, :], start=True, stop=True)
                    return ins
                S.op("pe", kvs, reads=(Km_b[bi], smp_b), writes=(PB_[2],))
                S.op("dve", lambda e: e.scalar_tensor_tensor(
                    out=stt[si][:].rearrange("p a b -> p (a b)"), in0=stt[si][:].rearrange("p a b -> p (a b)"),
                    scalar=float(gam[h]), in1=banks[2][:, 0:512], op0=ALU.mult, op1=ALU.add),
                    reads=(PB_[2], stt_b[si]), writes=(stt_b[si],))
                S.store(stt_b[si], ssn_o[b, h].rearrange("(c p) v -> p c v", p=128), stt[si])

            SPB = NS // NB
            Sb_ap = [Sbf, SbfB]
            Sb_bf = [Sbf_b, SbfB_b]
            bctx = {}
            sctx = {}
            for b_ in range(min(2, NS)):
                sctx[b_] = smp_A(b_)

            def rb_t0(n, h=h):
                cs_ = slice(n * 128, (n + 1) * 128)
                ai = n % 2
                aslot = banks[4][:, (n % 4) * 128:(n % 4 + 1) * 128]
                mm_group(4, aslot, [(kTh[:, dc, cs_], qTh[:, dc, cs_]) for dc in range(2)], reads=(kTh_b, qTh_b))
                S.op("dve", lambda e: e.tensor_tensor(out=attm[ai], in0=aslot, in1=dmk[:, h, :], op=ALU.mult),
                     reads=(PB_[4], dmk_b), writes=(attm_b[ai],))

                def kvf2(e):
                    ins = None
                    for dc in range(2):
                        ins = e.matmul(banks[6][:, dc * 256:(dc + 1) * 256], lhsT=ktok[:, n, dc * 128:(dc + 1) * 128],
                                       rhs=rv[:, n, :], start=True, stop=True)
                    return ins
                S.op("pe", kvf2, reads=(ktok_b, rv_b), writes=(PB_[6],))
                S.op("dve", lambda e: e.scalar_tensor_tensor(
                    out=Sst[:, h].rearrange("p a b -> p (a b)"), in0=Sst[:, h].rearrange("p a b -> p (a b)"),
                    scalar=float(cdec[h]), in1=banks[6][:, 0:512], op0=ALU.mult, op1=ALU.add),
                    reads=(PB_[6], Sst_b[h]), writes=(Sst_b[h],))
                if n < NB - 1:
                    nxt = (n + 1) % 2
                    S.op("act", lambda e: e.activation(out=Sb_ap[nxt][:, h], in_=Sst[:, h], func=AF.Copy),
                         reads=(Sst_b[h],), writes=(Sb_bf[nxt][h],))

            def rb_t1(n, h=h):
                cs_ = slice(n * 128, (n + 1) * 128)
                ai = n % 2
                cur = n % 2
                oslot = banks[5][:, (n % 2) * 256:(n % 2 + 1) * 256]
                mm_group(5, oslot, [(attm[ai], rv[:, n, :])] + [(qdT[:, dc, cs_], Sb_ap[cur][:, h, dc, :]) for dc in range(2)],
                         reads=(attm_b[ai], rv_b, qdT_b, Sb_bf[cur][h]))
                bctx[n] = rng_A(oslot, 5, 128, gg[:, n, :])

            def rb_t2(n, h=h):
                cs_ = slice(n * 128, (n + 1) * 128)
                rng_B(bctx.pop(n), 128, cs_, h)
                for b_ in range(n * SPB, (n + 1) * SPB):
                    smp_B(b_, sctx.pop(b_))
                    if b_ + 2 < NS:
                        sctx[b_ + 2] = smp_A(b_ + 2)
            if cfg.get("RB_SKEW", True):
                for t_ in range(NB + 2):
                    if 0 <= t_ - 1 < NB:
                        rb_t1(t_ - 1)
                    if t_ < NB:
                        rb_t0(t_)
                    if 0 <= t_ - 2 < NB:
                        rb_t2(t_ - 2)
                    if cfg.get("QK_OVERLAP", True):
                        if t_ % 2 == 0 and posts:
                            posts.pop(0)()
                        if t_ % 2 == 1 and next_chunks:
                            posts.append(next_chunks.pop(0)())
            else:
                skewed(NB, [rb_t0, rb_t1, rb_t2], False)
            S.store(Sst_b[h], sret_o[h].rearrange("(c p) v -> p c v", p=128), Sst[:, h])
            S.op("dve", lambda e: e.tensor_scalar(out=os32[:NS], in0=rvs32[:NS], scalar1=qk[:NS, 0:1], scalar2=None, op0=ALU.mult),
                 reads=(smp_b, qk_b), writes=(os_b,))
            S.op("dve", lambda e, h=h: e.scalar_tensor_tensor(out=banks[7][:NS, 256:512], in0=banks[7][:NS, 0:256], scalar=float(gam[h]),
                                                              in1=os32[:NS], op0=ALU.mult, op1=ALU.add),
                 reads=(PB_[7], os_b), writes=(PB_[7],))
            ret_norm_gate(banks[7][:NS, 256:512], 7, NS, gg[:NS, NB, :], slice(NT, NTS), h)

        posts = []
        first_chunks = qk_chunks(0, 0)
        for ch_ in first_chunks:
            ch_()()
        for h in range(RH):
            nxt_chunks = qk_chunks(h + 1, (h + 1) % 2) if h + 1 < RH else []
            head_body(h, h % 2, nxt_chunks)
            while posts:
                posts.pop(0)()
            for ch_ in nxt_chunks:
                ch_()()
        S.barrier()

        R1.clear(); R3.clear()
        acc = R3.alloc([128, NB + 1, D], F32)
        acc_b = [Buf(f"acc{b}") for b in range(NB + 1)]
        for b in range(NB):
            S.load(acc_b[b], acc[:, b, :], xown[b * 128:(b + 1) * 128, :])
        S.load(acc_b[NB], acc[:NS, NB, :], xsmp[:, :])
        NSC = NormScratch(R1, g2)
        wslot = [0]
        for t in range(D // 256):
            wt, wb_ = w_next()
            for b in range(NB + 1):
                cols, P = blk_cols(b)
                s = wslot[0] % 6
                wslot[0] += 1
                bk, half = s // 2, s % 2
                ps = banks[bk][:P, half * 256:(half + 1) * 256]
                mm_group(bk, ps, [(mixT[:, k, cols], wt[:, k, :]) for k in range(16)], reads=(wb_, mixT_b))
                S.op("dve", lambda e, ps=ps, P=P, b=b, t=t: e.tensor_tensor(out=acc[:P, b, t * 256:(t + 1) * 256], in0=ps,
                                                                            in1=acc[:P, b, t * 256:(t + 1) * 256], op=ALU.add),
                     reads=(PB_[bk], acc_b[b]), writes=(acc_b[b],))
        R2.clear()
        h2T = R2.alloc([128, KD, NTS], BF16)
        h2T_b = mixT_b
        items = []
        for b in range(NB + 1):
            cols, P = blk_cols(b)
            items.append(dict(P=P, src_sb=acc[:P, b, :], src_buf=acc_b[b],
                              dst_fn=(lambda k0, k1, cols=cols: h2T[:, k0:k1, cols]), dst_bufs=[h2T_b]))
        norm_seq(NSC, items)
        S.barrier()

        R1.clear()
        hid = [R1.alloc([128, 8, NTS], BF16) for _ in range(2)]
        hid_b = [Buf("hid0"), Buf("hid1")]
        rl = [R3.alloc([128, 512], F32) for _ in range(2)]
        rl_b = [Buf("rl0"), Buf("rl1")]
        tg_all = tgroups + [(NT, NS)]
        urot = [0]
        drot = [0]
        for fg in range(FG):
            hs = fg % 2
            for t in range(4):
                wt, wb_ = w_next()
                for fc in range(2):
                    ch = t * 2 + fc
                    for (t0, tn_) in tg_all:
                        bk = urot[0] % 3
                        ri = urot[0] % 2
                        urot[0] += 1
                        ps = banks[bk][:, 0:tn_]
                        mm_group(bk, ps, [(wt[:, k, fc * 128:(fc + 1) * 128], h2T[:, k, t0:t0 + tn_]) for k in range(KD)],
                                 reads=(wb_, h2T_b))
                        S.op("act", lambda e, ps=ps, ri=ri, tn_=tn_: e.activation(out=rl[ri][:, 0:tn_], in_=ps, func=AF.Relu),
                             reads=(PB_[bk],), writes=(rl_b[ri],))
                        S.op("pool", lambda e, ri=ri, tn_=tn_, hs=hs, ch=ch, t0=t0: e.tensor_tensor(
                            out=hid[hs][:, ch, t0:t0 + tn_], in0=rl[ri][:, 0:tn_], in1=rl[ri][:, 0:tn_], op=ALU.mult),
                            reads=(rl_b[ri],), writes=(hid_b[hs],))
            for t in range(max(1, D // 512)):
                wt, wb_ = w_next()
                for b in range(NB + 1):
                    cols, P = blk_cols(b)
                    bk = 3 + drot[0] % 3
                    drot[0] += 1
                    ps = banks[bk][:P, 0:DN]
                    mm_group(bk, ps, [(hid[hs][:, kk, cols], wt[:, kk, :]) for kk in range(8)], reads=(wb_, hid_b[hs]))
                    S.op("dve", lambda e, ps=ps, P=P, b=b, t=t: e.tensor_tensor(out=acc[:P, b, t * DN:(t + 1) * DN], in0=ps,
                                                                                in1=acc[:P, b, t * DN:(t + 1) * DN], op=ALU.add),
                         reads=(PB_[bk], acc_b[b]), writes=(acc_b[b],))
        for b in range(NB):
            S.store(acc_b[b], y_o[b * 128:(b + 1) * 128, :], acc[:, b, :])
        S.store(acc_b[NB], ys_o[:, :], acc[:NS, NB, :])
        S.finish()

        @block.sync
        def _(e):
            for f in S.prog["sp"]:
                f(e)

        @block.tensor
        def _(e):
            for f in S.prog["pe"]:
                f(e)

        @block.scalar
        def _(e):
            for f in S.prog["act"]:
                f(e)

        @block.vector
        def _(e):
            for f in S.prog["dve"]:
                f(e)

        @block.gpsimd
        def _(e):
            for f in S.prog["pool"]:
                f(e)

    return nc


def _w_in_cols():
    A_Q = 1024
    KVW = 128
    o_ak, o_av = A_Q, A_Q + KVW
    o_rq = A_Q + 2 * KVW
    o_rk = o_rq + 1024
    o_rv = o_rk + 1024
    o_rg = o_rv + 1024
    cols = list(range(0, A_Q)) + list(range(o_ak, o_ak + 128)) + list(range(o_av, o_av + 128))
    cols += list(range(o_ak, o_ak + 128))
    for h in range(RH):
        cols += list(range(o_rq + h * 256, o_rq + (h + 1) * 256))
        cols += list(range(o_rk + h * 256, o_rk + (h + 1) * 256))
        cols += list(range(o_rv + h * 256, o_rv + (h + 1) * 256))
        cols += list(range(o_rg + h * 256, o_rg + (h + 1) * 256))
    return np.asarray(cols)


def make_consts(cfg, core, seq_off):
    NB, NS, PB = cfg["NB"], cfg["NS"], cfg["PB"]
    NT = NB * 128
    gam = gammas()
    half = 128
    inv = (np.float32(ROPE_BASE) ** (-np.arange(half, dtype=np.float32) / np.float32(half))).astype(np.float32)

    def cs(pos):
        ang = pos.astype(np.float32)[None, :] * inv[:, None]
        return np.cos(ang).astype(np.float32), np.sin(ang).astype(np.float32)
    p0 = seq_off
    cT, sT = cs(np.arange(p0, p0 + NT))
    ppos = np.arange(p0 - PB * 128, p0)
    cP, sP = cs(np.maximum(ppos, 0))
    cS, sS = cs(np.full((NS,), PAST_LEN))
    cS, sS = np.ascontiguousarray(cS.T), np.ascontiguousarray(sS.T)
    i = np.arange(128)
    dm = np.zeros((128, RH, 128), np.float32)
    kd = np.zeros((128, RH), np.float32)
    qd = np.zeros((128, RH, 128), np.float32)
    for h in range(RH):
        lg = math.log1p(-2.0 ** (-5.0 - h))
        rel = i[None, :] - i[:, None]
        dm[:, h, :] = np.where(rel >= 0, np.exp(lg * np.maximum(rel, 0)), 0.0) / 16.0
        kd[:, h] = np.exp(lg * (127.0 - i)) / 16.0
        qd[:, h, :] = np.exp(lg * (i + 1.0))[None, :]
    own = (i[:, None] <= i[None, :]).astype(np.float32)
    prv = (i[:, None] >= i[None, :]).astype(np.float32)
    prv_first = prv if core > 0 else np.zeros_like(prv)
    mk = np.stack([np.stack([own, prv_first, own, prv_first], 1), np.stack([own, prv, own, prv], 1)], 1)
    eye = np.eye(NS, dtype=np.float32)
    sel = np.zeros((NS, NS, 128), np.float32)
    for b in range(NS):
        sel[b, b, :] = 1.0
    GP = cfg["GP"]
    kd4 = np.zeros((128, RH, GP), np.float32)
    for h in range(RH):
        lg = math.log1p(-2.0 ** (-5.0 - h))
        for j in range(GP):
            kd4[:, h, j] = np.exp(lg * (127.0 - i + 128.0 * (GP - 1 - j))) / 16.0
    return dict(cosT=cT, sinT=sT, cosP=cP, sinP=sP, cosS=cS, sinS=sS, dmaskT=dm, kdec=kd, qdec=qd, kdec4=kd4,
                masks=mk.astype(ml_dtypes.bfloat16), ident=np.eye(128, dtype=np.float32).astype(ml_dtypes.bfloat16),
                eye=np.ascontiguousarray(np.broadcast_to(eye[None], (128, NS, NS))), eyeT=eye,
                sel=sel.astype(ml_dtypes.bfloat16))


def make_in_maps(cfg, inp):
    NB, NS, PB, NC = cfg["NB"], cfg["NS"], cfg["PB"], cfg["NCORES"]
    NT = NB * 128
    xp = np.asarray(inp["x_prompt"], np.float32)[0]
    xs = np.asarray(inp["x_sample"], np.float32)[:, 0]
    D = xp.shape[1]
    w_in = np.ascontiguousarray(np.asarray(inp["w_in"], np.float32)[0][:, _w_in_cols()])
    shared = dict(
        w_in=w_in,
        w_out=np.ascontiguousarray(np.asarray(inp["w_out"], np.float32)[0]),
        w_up=np.ascontiguousarray(np.asarray(inp["w_up"], np.float32)[0]),
        w_down=np.ascontiguousarray(np.asarray(inp["w_down"], np.float32)[0]),
        g1=np.ascontiguousarray(np.asarray(inp["ln1_g"], np.float32)[0]),
        g2=np.ascontiguousarray(np.asarray(inp["ln2_g"], np.float32)[0]),
        gq=np.ascontiguousarray(np.asarray(inp["q_norm_g"], np.float32)[0]),
        gk=np.ascontiguousarray(np.asarray(inp["k_norm_g"], np.float32)[0]),
        sinks=np.ascontiguousarray(np.asarray(inp["attn_sinks"], np.float32)[0]),
        gret=np.ascontiguousarray(np.asarray(inp["ret_norm_g"], np.float32)[0]),
    )
    ckw = np.asarray(inp["cache_k_win"], np.float32)[0]
    cvw = np.asarray(inp["cache_v_win"], np.float32)[0]
    st = np.asarray(inp["state_ret"], np.float32)[0]
    maps = []
    for c in range(NC):
        p0 = c * NT
        xprev = np.zeros((max(PB, 1) * 128, D), np.float32)
        if PB > 0:
            lo = p0 - PB * 128
            src_lo = max(lo, 0)
            if p0 > src_lo:
                xprev[src_lo - lo:] = xp[src_lo:p0]
        m = dict(shared)
        m.update(make_consts(cfg, c, p0))
        m.update(
            xprev=xprev[:PB * 128] if PB > 0 else xprev,
            xown=np.ascontiguousarray(xp[p0:p0 + NT]),
            xsmp=np.ascontiguousarray(xs[c * NS:(c + 1) * NS]),
            ck=np.ascontiguousarray(ckw[c * NS:(c + 1) * NS].reshape(NS, 128, 128)),
            cv=np.ascontiguousarray(cvw[c * NS:(c + 1) * NS].reshape(NS, 128, 128)),
            st0=np.ascontiguousarray(st[c * NS:(c + 1) * NS]),
        )
        maps.append(m)
    return maps


def assemble(cfg, results):
    NC = cfg["NCORES"]
    y = np.concatenate([r["y"] for r in results], 0)[None]
    ys = np.concatenate([r["ys"] for r in results], 0)[:, None, :]
    last = results[NC - 1]
    kwin = last["kwin"].reshape(1, 1, 128, 2, 64)
    vwin = last["vwin"].reshape(1, 1, 128, 2, 64)
    sret = last["sret"].reshape(1, 1, RH, 256, 256)
    ksw = np.concatenate([r["ksw"] for r in results], 0).reshape(1, -1, 128, 2, 64)
    vsw = np.concatenate([r["vsw"] for r in results], 0).reshape(1, -1, 128, 2, 64)
    ssn = np.concatenate([r["ssn"] for r in results], 0)[None]
    return tuple(np.ascontiguousarray(a, dtype=np.float32) for a in (y, ys, kwin, vwin, sret, ksw, vsw, ssn))


def kernel(**inputs):
    cfg = FULL_CFG
    nc = build_nc(cfg)
    in_maps = make_in_maps(cfg, inputs)
    res = run_bass_kernel_spmd(nc, in_maps, core_ids=list(range(cfg["NCORES"])))
    return assemble(cfg, res.results)
```

```python
import math
import numpy as np
import ml_dtypes
import concourse.bass as bass
import concourse.mybir as mybir
from concourse.bass_utils import run_bass_kernel_spmd

F32 = mybir.dt.float32
BF16 = mybir.dt.bfloat16
ALU = mybir.AluOpType
AF = mybir.ActivationFunctionType
AX = mybir.AxisListType

EPS = 1e-6
ATT_HEADS, ATT_KV, HD = 16, 2, 64
RH, DK, DV = 4, 256, 256
PAST_LEN = 8192
ROPE_BASE = 10000.0
N_CORES = 8

FULL_CFG = dict(KD=16, NB=8, NS=16, PB=36, GP=4, NPRE=(8, 12, 20, 36), FG=8, NCORES=8)


class Buf:
    __slots__ = ("name", "w", "r", "lsem", "ssem")

    def __init__(self, name):
        self.name = name
        self.w = None
        self.r = {}
        self.lsem = None
        self.ssem = None


class DSem:
    __slots__ = ("h", "count")

    def __init__(self, h):
        self.h = h
        self.count = 0


ENGS = ("pe", "act", "dve", "pool", "sp")


class Sched:
    def __init__(self, nc, psems, dsem_pool):
        self.nc = nc
        self.psem = psems
        self.prog = {e: [] for e in ENGS}
        self.cnt = {e: 0 for e in ENGS}
        self.seen = {e: {} for e in ENGS}
        self.dpool = dsem_pool
        self.dnext = 0
        self.finals = []
        self.all_dma = {}

    def new_dsem(self):
        h = self.dpool(self.dnext)
        self.dnext += 1
        return DSem(h)

    def _waits(self, eng, reads, writes):
        need = {}

        def add(tok):
            sem, val, seng = tok
            if seng == "pe" and eng == "pe":
                return
            k = sem.num
            if k not in need or need[k][1] < val:
                need[k] = (sem, val)

        for b in reads:
            if b.w is not None:
                add(b.w)
        for b in writes:
            if b.w is not None:
                add(b.w)
            for t in b.r.values():
                add(t)
        out = []
        seen = self.seen[eng]
        for k, (sem, val) in need.items():
            if seen.get(k, 0) < val:
                seen[k] = val
                out.append((sem, val))
        return out

    def _mark(self, tok, reads, writes):
        k = tok[0].num
        for b in reads:
            old = b.r.get(k)
            if old is None or old[1] < tok[1]:
                b.r[k] = tok
        for b in writes:
            b.w = tok
            b.r = {}

    def op(self, eng, fn, reads=(), writes=()):
        waits = self._waits(eng, reads, writes)
        self.cnt[eng] += 1
        sem = self.psem[eng]
        tok = (sem, self.cnt[eng], eng)

        def run(e, fn=fn, waits=waits, sem=sem):
            for s, v in waits:
                e.wait_ge(s, v)
            ins = fn(e)
            ins.then_inc(sem, 1)

        self.prog[eng].append(run)
        self._mark(tok, reads, writes)
        self.seen[eng][sem.num] = max(self.seen[eng].get(sem.num, 0), 0)
        return tok

    def dma(self, out, in_, reads=(), writes=(), dsem=None, final=False, q="sp"):
        waits = self._waits(q, reads, writes)
        if dsem is None:
            dsem = self.new_dsem()
        dsem.count += 16
        tok = (dsem.h, dsem.count, "dma")

        def run(e, out=out, in_=in_, waits=waits, h=dsem.h):
            for s, v in waits:
                e.wait_ge(s, v)
            e.dma_start(out=out, in_=in_).then_inc(h, 16)

        self.prog[q].append(run)
        self._mark(tok, reads, writes)
        self.all_dma[dsem.h.num] = (dsem.h, dsem.count)
        if final:
            self.finals.append(tok)
        return tok

    def load(self, buf, out, in_, reads=()):
        if buf.lsem is None:
            buf.lsem = self.new_dsem()
        return self.dma(out, in_, reads=reads, writes=(buf,), dsem=buf.lsem)

    def store(self, buf, out, in_, final=True):
        if buf.ssem is None:
            buf.ssem = self.new_dsem()
        return self.dma(out, in_, reads=(buf,), writes=(), dsem=buf.ssem, final=final)

    def barrier(self):
        targets = [(self.psem[e], self.cnt[e]) for e in ENGS if e != "sp" and self.cnt[e] > 0]
        targets += list(self.all_dma.values())
        for eng in ENGS:
            ws = []
            seen = self.seen[eng]
            for sem, val in targets:
                if sem.num == self.psem[eng].num and eng != "sp":
                    pass
                if seen.get(sem.num, 0) < val:
                    seen[sem.num] = val
                    ws.append((sem, val))
            if ws:
                def run(e, ws=ws):
                    for s, v in ws:
                        e.wait_ge(s, v)
                self.prog[eng].append(run)

    def finish(self):
        ws = [(self.psem[e], self.cnt[e]) for e in ENGS if e != "sp" and self.cnt[e] > 0]
        ws += list(self.all_dma.values())

        def run(e, ws=ws):
            for s, v in ws:
                e.wait_ge(s, v)
        self.prog["sp"].append(run)


class Arena:
    def __init__(self, tensor, nbytes, base=0):
        self.t = tensor
        self.base = base
        self.n = base + nbytes
        self.off = base

    def sub(self, nbytes):
        nb = (nbytes + 31) // 32 * 32
        assert self.off + nb <= self.n, f"arena overflow (sub) {self.off}+{nb}>{self.n}"
        a = Arena(self.t, nb, self.off)
        self.off += nb
        return a

    def rest(self):
        a = Arena(self.t, self.n - self.off, self.off)
        return a

    def clear(self):
        self.off = self.base

    def mark(self):
        return self.off

    def reset(self, m):
        self.off = m

    def alloc(self, shape, dtype):
        esz = 4 if dtype == F32 else 2
        n = 1
        for s in shape[1:]:
            n *= s
        nb = (n * esz + 31) // 32 * 32
        assert self.off + nb <= self.n, f"arena overflow {self.off}+{nb}>{self.n}"
        a = self.t[0:shape[0], self.off // 4:(self.off + nb) // 4]
        self.off += nb
        if dtype != F32:
            a = a.bitcast(dtype)
        a = a[:, 0:n]
        if len(shape) == 3:
            a = a.rearrange("p (a b) -> p a b", a=shape[1])
        elif len(shape) == 4:
            a = a.rearrange("p (a b c) -> p a b c", a=shape[1], b=shape[2])
        elif len(shape) == 5:
            a = a.rearrange("p (a b c d) -> p a b c d", a=shape[1], b=shape[2], c=shape[3])
        return a


def gammas():
    return [1.0 - 2.0 ** (-5.0 - h) for h in range(RH)]


def build_nc(cfg):
    NORM_AHEAD = cfg.get("NORM_AHEAD", True)
    KD, NB, NS, PB, GP, NPRE, FG = (cfg[k] for k in ("KD", "NB", "NS", "PB", "GP", "NPRE", "FG"))
    D = KD * 128
    DFF = FG * 1024
    NT = NB * 128
    NTS = NT + NS
    NKB = NB + 1
    WIN = 1408 + 4096
    gam = gammas()
    cdec = [g ** 128 for g in gam]

    nc = bass.Bass("TRN2", target_bir_lowering=False)

    def din(name, shape, dt=F32):
        return nc.dram_tensor(name, list(shape), dt, kind="ExternalInput").ap()

    def dout(name, shape, dt=F32):
        return nc.dram_tensor(name, list(shape), dt, kind="ExternalOutput").ap()

    xprev = din("xprev", [max(PB, 1) * 128, D])
    xown = din("xown", [NT, D])
    xsmp = din("xsmp", [NS, D])
    ck = din("ck", [NS, 128, 128])
    cv = din("cv", [NS, 128, 128])
    st0 = din("st0", [NS, RH, 256, 256])
    w_in = din("w_in", [D, WIN])
    w_out = din("w_out", [2048, D])
    w_up = din("w_up", [D, DFF])
    w_down = din("w_down", [DFF, D])
    g1 = din("g1", [D])
    g2 = din("g2", [D])
    gq = din("gq", [64])
    gk = din("gk", [64])
    sinks = din("sinks", [16])
    gret = din("gret", [1024])
    cosT = din("cosT", [128, NT])
    sinT = din("sinT", [128, NT])
    cosP = din("cosP", [128, max(PB, 1) * 128])
    sinP = din("sinP", [128, max(PB, 1) * 128])
    cosS = din("cosS", [NS, 128])
    sinS = din("sinS", [NS, 128])
    dmaskT = din("dmaskT", [128, RH, 128])
    kdec = din("kdec", [128, RH])
    qdec = din("qdec", [128, RH, 128])
    kdec4 = din("kdec4", [128, RH, GP])
    masks = din("masks", [128, 2, 4, 128], BF16)
    ident_d = din("ident", [128, 128], BF16)
    eye_d = din("eye", [128, NS, NS])
    eyeT_d = din("eyeT", [NS, NS])
    sel_d = din("sel", [NS, NS, 128], BF16)

    y_o = dout("y", [NT, D])
    ys_o = dout("ys", [NS, D])
    kwin_o = dout("kwin", [128, 128])
    vwin_o = dout("vwin", [128, 128])
    sret_o = dout("sret", [RH, 256, 256])
    ksw_o = dout("ksw", [NS, 128, 128])
    vsw_o = dout("vsw", [NS, 128, 128])
    ssn_o = dout("ssn", [NS, RH, 256, 256])

    ARENA_BYTES = 206 * 1024
    import contextlib
    with contextlib.ExitStack() as es:
        arena_t = es.enter_context(nc.sbuf_tensor("arena", [128, ARENA_BYTES // 4], F32))
        banks = [es.enter_context(nc.psum_tensor(f"bank{i}", [128, 512], F32)) for i in range(8)]
        psems = {e: es.enter_context(nc.semaphore(f"prog_{e}")) for e in ENGS}
        dpool = lambda i: es.enter_context(nc.semaphore(f"dsem{i}"))
        block = es.enter_context(nc.Block())

        S = Sched(nc, psems, dpool)
        P0 = Arena(arena_t, ARENA_BYTES)
        PB_ = [Buf(f"bank{i}") for i in range(8)]

        def bank_bf(i):
            return banks[i][:].bitcast(BF16)

        def const(A, shape, dtype, src, name):
            t = A.alloc(shape, dtype)
            b = Buf(name)
            S.load(b, t, src)
            return t, b

        ident, ident_b = const(P0, [128, 128], BF16, ident_d, "ident")
        gqb, gqb_b = const(P0, [128, 64], F32, gq.partition_broadcast(128), "gqb")
        gkb, gkb_b = const(P0, [128, 64], F32, gk.partition_broadcast(128), "gkb")
        snk, snk_b = const(P0, [128, 16], F32, sinks.partition_broadcast(128), "snk")
        gretb, gretb_b = const(P0, [128, 1024], F32, gret.partition_broadcast(128), "gretb")
        dmk, dmk_b = const(P0, [128, RH, 128], F32, dmaskT, "dmk")
        kdc, kdc_b = const(P0, [128, RH], F32, kdec, "kdc")
        qdc, qdc_b = const(P0, [128, RH, 128], F32, qdec, "qdc")
        eye, eye_b = const(P0, [128, NS, NS], F32, eye_d, "eye")
        eyeT, eyeT_b = const(P0, [NS, NS], F32, eyeT_d, "eyeT")
        cSs, cSs_b = const(P0, [NS, 128], F32, cosS, "cosS")
        sSs, sSs_b = const(P0, [NS, 128], F32, sinS, "sinS")

        mhalf = P0.alloc([128, 16], F32)
        ones_bf = P0.alloc([128, 16], BF16)
        cst_b = Buf("cst")
        S.op("pool", lambda e: e.memset(mhalf, -0.5), writes=(cst_b,))
        S.op("pool", lambda e: e.memset(ones_bf, 1.0), writes=(cst_b,))
        esnk = P0.alloc([128, 16], F32)
        esnk_b = Buf("esnk")
        S.op("act", lambda e: e.activation(out=esnk, in_=snk, func=AF.Exp), reads=(snk_b,), writes=(esnk_b,))

        Sst = P0.alloc([128, RH, 2, 256], F32)
        Sbf = P0.alloc([128, RH, 2, 256], BF16)
        Sst_b = [Buf(f"S{h}") for h in range(RH)]
        Sbf_b = [Buf(f"Sbf{h}") for h in range(RH)]
        for h in range(RH):
            S.op("pool", lambda e, h=h: e.memset(Sst[:, h], 0.0), writes=(Sst_b[h],))
            S.op("pool", lambda e, h=h: e.memset(Sbf[:, h], 0.0), writes=(Sbf_b[h],))

        NSTAT = 48
        stat = P0.alloc([128, NSTAT, 8], F32)
        stat_b = [Buf(f"stat{i}") for i in range(NSTAT)]
        stat_i = [0]

        def new_stat():
            i = stat_i[0] % NSTAT
            stat_i[0] += 1
            return stat[:, i], stat_b[i]

        junk = P0.alloc([128, 256], BF16)
        junk_b = Buf("junk")
        hTh = P0.alloc([128, KD, 128], BF16)
        hTh_b = Buf("hTh")

        NSTG, NWB = 4, 2
        stg = [P0.alloc([128, 1024], F32) for _ in range(NSTG)]
        stg_b = [Buf(f"stg{i}") for i in range(NSTG)]
        wbf = [P0.alloc([128, 4096], BF16) for _ in range(NWB)]
        wbf_b = [Buf(f"wbf{i}") for i in range(NWB)]
        wtiles = []
        wstate = dict(loaded=0, cast=0, used=0, piece=0)
        cast_rot = cfg.get("CAST_ROT", ["act", "dve", "act", "pool"])

        def w_view_rows(w, r0, kk, c0, nn):
            return w[r0:r0 + kk * 128, c0:c0 + nn].rearrange("(k p) n -> p k n", p=128)

        def w_pieces(kk):
            nq = min(NSTG, kk)
            return nq, kk // nq

        def w_issue_load(i):
            view, kk, nn = wtiles[i]
            nq, hk = w_pieces(kk)
            for q in range(nq):
                dst = stg[q][:, 0:hk * nn].rearrange("p (k n) -> p k n", k=hk)
                S.load(stg_b[q], dst, view[:, q * hk:(q + 1) * hk, :])

        def w_issue_cast(i):
            view, kk, nn = wtiles[i]
            nq, hk = w_pieces(kk)
            wb = i % NWB
            for q in range(nq):
                src = stg[q][:, 0:hk * nn]
                dst = wbf[wb][:, q * hk * nn:(q + 1) * hk * nn]
                eng = cast_rot[wstate["piece"] % len(cast_rot)]
                wstate["piece"] += 1
                if eng == "act":
                    S.op("act", lambda e, d=dst, s_=src: e.activation(out=d, in_=s_, func=AF.Copy),
                         reads=(stg_b[q],), writes=(wbf_b[wb],))
                else:
                    S.op(eng, lambda e, d=dst, s_=src: e.tensor_copy(out=d, in_=s_),
                         reads=(stg_b[q],), writes=(wbf_b[wb],))

        def w_next():
            i = wstate["used"]
            wstate["used"] += 1
            while wstate["cast"] < min(len(wtiles), i + 2):
                j = wstate["cast"]
                if wstate["loaded"] <= j:
                    w_issue_load(j)
                    wstate["loaded"] = j + 1
                w_issue_cast(j)
                wstate["cast"] += 1
            view, kk, nn = wtiles[i]
            wb = i % NWB
            return wbf[wb][:, 0:kk * nn].rearrange("p (k n) -> p k n", k=kk), wbf_b[wb]

        IN_A_Q, IN_A_KV, IN_B = 0, 1024, 1408
        pre_passes = [[2, 3], [0, 1]] if PB > 0 else []
        order = []
        for hs_ in pre_passes:
            for h in hs_:
                order.append(("pre_rk", h))
                order.append(("pre_rv", h))
        for t in range(4):
            order.append(("aq", t))
        order.append(("akv", 0))
        for h in range(RH):
            for nm in ("rq", "rk", "rv", "rg"):
                order.append((nm, h))
        for t in range(D // 256):
            order.append(("wout", t))
        for fg in range(FG):
            for t in range(4):
                order.append(("up", fg * 4 + t))
            for t in range(max(1, D // 512)):
                order.append(("down", fg, t))
        DN = min(512, D)
        for it in order:
            nm = it[0]
            if nm in ("pre_rk", "rk"):
                wtiles.append((w_view_rows(w_in, 0, KD, IN_B + it[1] * 1024 + 256, 256), KD, 256))
            elif nm in ("pre_rv", "rv"):
                wtiles.append((w_view_rows(w_in, 0, KD, IN_B + it[1] * 1024 + 512, 256), KD, 256))
            elif nm == "rq":
                wtiles.append((w_view_rows(w_in, 0, KD, IN_B + it[1] * 1024, 256), KD, 256))
            elif nm == "rg":
                wtiles.append((w_view_rows(w_in, 0, KD, IN_B + it[1] * 1024 + 768, 256), KD, 256))
            elif nm == "aq":
                wtiles.append((w_view_rows(w_in, 0, KD, IN_A_Q + it[1] * 256, 256), KD, 256))
            elif nm == "akv":
                wtiles.append((w_view_rows(w_in, 0, KD, IN_A_KV, 256), KD, 256))
            elif nm == "wout":
                wtiles.append((w_view_rows(w_out, 0, 16, it[1] * 256, 256), 16, 256))
            elif nm == "up":
                wtiles.append((w_view_rows(w_up, 0, KD, it[1] * 256, 256), KD, 256))
            elif nm == "down":
                wtiles.append((w_view_rows(w_down, it[1] * 1024, 8, it[2] * DN, DN), 8, DN))

        RR = P0.rest()
        R1_BYTES = 16 * NTS * 2
        R1 = RR.sub(R1_BYTES)
        R2 = RR.sub(R1_BYTES)
        R3 = RR.rest()
        RRall = Arena(arena_t, RR.n - RR.base, RR.base)

        def rstd_from_ssq(ssq_ap, sb, P, n, ncols=1):
            S.op("pool", lambda e: e.tensor_scalar(out=ssq_ap, in0=ssq_ap, scalar1=1.0 / n, scalar2=EPS,
                                                   op0=ALU.mult, op1=ALU.add), reads=(sb,), writes=(sb,))
            S.op("pool", lambda e: e.tensor_tensor(out=ssq_ap, in0=ssq_ap, in1=mhalf[:P, 0:ncols], op=ALU.pow),
                 reads=(sb, cst_b), writes=(sb,))

        class NormScratch:
            def __init__(self, A, gsrc):
                self.xt = [A.alloc([128, D], F32) for _ in range(2)]
                self.xt_b = [Buf("xt0"), Buf("xt1")]
                self.xg = [A.alloc([128, D], BF16) for _ in range(2)]
                self.xg_b = [Buf("xg0"), Buf("xg1")]
                self.gb, self.gb_b = const(A, [128, D], F32, gsrc.partition_broadcast(128), "gb")
                self.i = 0

        def norm_A(NSC, P, src_dram=None, src_sb=None, src_buf=None):
            i = NSC.i % 2
            NSC.i += 1
            xt, xt_b, xg, xg_b = NSC.xt, NSC.xt_b, NSC.xg, NSC.xg_b
            if src_dram is not None:
                S.load(xt_b[i], xt[i][:P], src_dram)
                src, sbuf_ = xt[i][:P], xt_b[i]
            else:
                src, sbuf_ = src_sb, src_buf
            st_, st_bf = new_stat()
            S.op("act", lambda e: e.activation(out=xg[i][:P], in_=src, func=AF.Square, accum_out=st_[:P, 0:1]),
                 reads=(sbuf_,), writes=(xg_b[i], st_bf))
            rstd_from_ssq(st_[:P, 0:1], st_bf, P, D)
            S.op("dve", lambda e: e.scalar_tensor_tensor(out=xg[i][:P], in0=src, scalar=st_[:P, 0:1], in1=NSC.gb[:P],
                                                         op0=ALU.mult, op1=ALU.mult),
                 reads=(sbuf_, st_bf, NSC.gb_b), writes=(xg_b[i],))
            return (NSC, P, i)

        def norm_B(ctx, dst_fn, dst_bufs, tb=(6, 7)):
            NSC, P, i = ctx
            xg, xg_b = NSC.xg, NSC.xg_b
            nbank = (KD + 7) // 8
            for bi in range(nbank):
                k0, k1 = bi * 8, min(KD, bi * 8 + 8)
                bk = tb[bi % 2]
                pv = bank_bf(bk)[:, 0:(k1 - k0) * 128].rearrange("p (k t) -> p k t", k=k1 - k0)

                def tr(e, k0=k0, k1=k1, pv=pv, i=i):
                    ins = None
                    for k in range(k0, k1):
                        ins = e.transpose(out=pv[:, k - k0, 0:P], in_=xg[i][:P, k * 128:(k + 1) * 128],
                                          identity=ident[:P, :P])
                    return ins
                S.op("pe", tr, reads=(xg_b[i], ident_b), writes=(PB_[bk],))
                d = dst_fn(k0, k1)
                S.op("act", lambda e, d=d, pv=pv: e.activation(out=d, in_=pv[:, :, 0:P], func=AF.Copy),
                     reads=(PB_[bk],), writes=tuple(dst_bufs))

        def norm_seq(NSC, items):
            ctxs = {}
            if items:
                it = items[0]
                ctxs[0] = norm_A(NSC, it["P"], it.get("src_dram"), it.get("src_sb"), it.get("src_buf"))
            for n, it in enumerate(items):
                if NORM_AHEAD and n + 1 < len(items):
                    nx = items[n + 1]
                    ctxs[n + 1] = norm_A(NSC, nx["P"], nx.get("src_dram"), nx.get("src_sb"), nx.get("src_buf"))
                if n not in ctxs:
                    ctxs[n] = norm_A(NSC, it["P"], it.get("src_dram"), it.get("src_sb"), it.get("src_buf"))
                norm_B(ctxs.pop(n), it["dst_fn"], it["dst_bufs"])
                if it.get("after") is not None:
                    it["after"]()

        def mm_group(bank_i, out_ap, pairs, reads, extra_writes=()):
            def f(e):
                ins = None
                n = len(pairs)
                for j, (l, r) in enumerate(pairs):
                    ins = e.matmul(out_ap, lhsT=l, rhs=r, start=(j == 0), stop=(j == n - 1))
                return ins
            return S.op("pe", f, reads=tuple(reads), writes=(PB_[bank_i],) + tuple(extra_writes))

        def rotary(x1p, x2p, pbufs, cos_ap, sin_ap, tbufs, o1, o2, obufs, rt, rt_b):
            t1, t2, t3, t4 = rt
            rd = tuple(pbufs) + tuple(tbufs)
            S.op("dve", lambda e: e.tensor_tensor(out=t1, in0=x1p, in1=cos_ap, op=ALU.mult), reads=rd, writes=(rt_b[0],))
            S.op("dve", lambda e: e.tensor_tensor(out=t2, in0=x2p, in1=sin_ap, op=ALU.mult), reads=rd, writes=(rt_b[1],))
            S.op("dve", lambda e: e.tensor_tensor(out=t3, in0=x2p, in1=cos_ap, op=ALU.mult), reads=rd, writes=(rt_b[2],))
            S.op("dve", lambda e: e.tensor_tensor(out=t4, in0=x1p, in1=sin_ap, op=ALU.mult), reads=rd, writes=(rt_b[3],))
            S.op("pool", lambda e: e.tensor_tensor(out=o1, in0=t1, in1=t2, op=ALU.subtract),
                 reads=(rt_b[0], rt_b[1]), writes=tuple(obufs))
            S.op("pool", lambda e: e.tensor_tensor(out=o2, in0=t3, in1=t4, op=ALU.add),
                 reads=(rt_b[2], rt_b[3]), writes=tuple(obufs))

        if PB > 0:
            A = RRall
            A.clear()
            GW = GP * 128
            NSC = NormScratch(A, g1)
            hTg = [A.alloc([128, KD, GW], BF16) for _ in range(2)]
            hTg_b = [Buf(f"hTg{i}") for i in range(2)]
            cpt = [A.alloc([128, GW], F32) for _ in range(2)]
            spt = [A.alloc([128, GW], F32) for _ in range(2)]
            cpt_b = [Buf(f"cpt{i}") for i in range(2)]
            rt2 = [[A.alloc([128, GW], F32) for _ in range(4)] for _ in range(2)]
            rt2_b = [[Buf(f"rt{s}_{i}") for i in range(4)] for s in range(2)]
            pw = [A.alloc([128, KD, 256], BF16) for _ in range(4)]
            pw_b = [Buf(f"pw{i}") for i in range(4)]
            kTg2 = [A.alloc([128, 2, GW], BF16) for _ in range(2)]
            kTg2_b = [Buf("kTg0"), Buf("kTg1")]
            ktok_g2 = [A.alloc([128, GP, 256], BF16) for _ in range(2)]
            ktok_g2b = [Buf("ktokg0"), Buf("ktokg1")]
            rv_g2 = [A.alloc([128, GP, 256], BF16) for _ in range(2)]
            rv_g2b = [Buf("rvg0"), Buf("rvg1")]
            kd4, kd4_b = const(A, [128, RH, GP], F32, kdec4, "kd4")
            hg_i = [0]
            ngroups = PB // GP
            gcount = 0
            for pi_, heads in enumerate(pre_passes):
                for j, h in enumerate(heads):
                    for w_ in range(2):
                        wt, wb_ = w_next()
                        eng = "pool" if w_ == 0 else "dve"
                        S.op(eng, lambda e, j=j, w_=w_, wt=wt: e.tensor_copy(out=pw[2 * j + w_], in_=wt),
                             reads=(wb_,), writes=(pw_b[2 * j + w_],))
                g_first = min((PB - NPRE[h]) // GP for h in heads)
                gi_of = {}
                for g in range(g_first, ngroups):
                    gi_of[g] = gcount % 2
                    gcount += 1

                def head_pieces(g, j_h, h):
                    gi = gi_of[g]
                    wrk, wrv = pw[2 * j_h], pw[2 * j_h + 1]
                    wrk_b, wrv_b = pw_b[2 * j_h], pw_b[2 * j_h + 1]
                    sl = hg_i[0] % 2
                    hg_i[0] += 1
                    rt, rt_b = rt2[sl], rt2_b[sl]
                    kTg, kTg_b = kTg2[sl], kTg2_b[sl]
                    ktok_g, ktok_gb = ktok_g2[sl], ktok_g2b[sl]
                    rv_g, rv_gb = rv_g2[sl], rv_g2b[sl]

                    def p_rk():
                        for dc in range(2):
                            mm_group(dc, banks[dc][:, 0:GW],
                                     [(wrk[:, k, dc * 128:(dc + 1) * 128], hTg[gi][:, k, :]) for k in range(KD)],
                                     reads=(wrk_b, hTg_b[gi]))
                        rotary(banks[0][:, 0:GW], banks[1][:, 0:GW], (PB_[0], PB_[1]), cpt[gi], spt[gi], (cpt_b[gi],),
                               kTg[:, 0, :], kTg[:, 1, :], (kTg_b,), rt, rt_b)

                    def p_rv(j2):
                        bk = 3 + (j2 // 2) % 2
                        nj = min(2, GP - j2)
                        for jj in range(nj):
                            j = j2 + jj
                            mm_group(bk, banks[bk][:, jj * 256:(jj + 1) * 256],
                                     [(hTg[gi][:, k, j * 128:(j + 1) * 128], wrv[:, k, :]) for k in range(KD)],
                                     reads=(wrv_b, hTg_b[gi]))
                        S.op("act", lambda e: e.activation(
                            out=rv_g[:, j2:j2 + nj, :], in_=banks[bk][:, 0:nj * 256].rearrange("p (j c) -> p j c", j=nj),
                            func=AF.Copy), reads=(PB_[bk],), writes=(rv_gb,))

                    def p_tail():
                        pv = bank_bf(2)[:, 0:GP * 256].rearrange("p (j c) -> p j c", j=GP)

                        def trk(e):
                            ins = None
                            for j in range(GP):
                                for dc in range(2):
                                    ins = e.transpose(out=pv[:, j, dc * 128:(dc + 1) * 128],
                                                      in_=kTg[:, dc, j * 128:(j + 1) * 128], identity=ident)
                            return ins
                        S.op("pe", trk, reads=(kTg_b, ident_b), writes=(PB_[2],))
                        for j in range(GP):
                            S.op("act", lambda e, j=j: e.activation(
                                out=ktok_g[:, j, :], in_=pv[:, j, :], func=AF.Copy, scale=kd4[:, h, j:j + 1]),
                                reads=(PB_[2], kd4_b), writes=(ktok_gb,))

                        def kvf(e):
                            ins = None
                            for dc in range(2):
                                for j in range(GP):
                                    ins = e.matmul(banks[5][:, dc * 256:(dc + 1) * 256],
                                                   lhsT=ktok_g[:, j, dc * 128:(dc + 1) * 128], rhs=rv_g[:, j, :],
                                                   start=(j == 0), stop=(j == GP - 1))
                            return ins
                        S.op("pe", kvf, reads=(ktok_gb, rv_gb), writes=(PB_[5],))
                        S.op("dve", lambda e: e.scalar_tensor_tensor(
                            out=Sst[:, h].rearrange("p a b -> p (a b)"), in0=Sst[:, h].rearrange("p a b -> p (a b)"),
                            scalar=float(cdec[h] ** GP), in1=banks[5][:, 0:512], op0=ALU.mult, op1=ALU.add),
                            reads=(PB_[5], Sst_b[h]), writes=(Sst_b[h],))
                    ps_ = [p_rk] + [(lambda j2=j2: p_rv(j2)) for j2 in range(0, GP, 2)] + [p_tail]
                    return ps_

                def heads_pieces(g, heads=heads):
                    out = []
                    for j_h, h in enumerate(heads):
                        if g * GP < PB - NPRE[h]:
                            continue
                        out.extend(head_pieces(g, j_h, h))
                    return out

                pend = []

                def run_pieces(frac):
                    n = len(pend)
                    if n == 0:
                        return
                    k = max(1, (n + frac - 1) // frac)
                    for _ in range(min(k, n)):
                        pend.pop(0)()

                items = []
                for g in range(g_first, ngroups):
                    gi = gi_of[g]
                    pb0 = g * GP
                    for j in range(GP):
                        pb = pb0 + j
                        it = dict(P=128, src_dram=xprev[pb * 128:(pb + 1) * 128, :],
                                  dst_fn=(lambda k0, k1, gi=gi, j=j: hTg[gi][:, k0:k1, j * 128:(j + 1) * 128]),
                                  dst_bufs=[hTg_b[gi]], after=None)
                        def aft(g=g, gi=gi, pi_=pi_, j=j):
                            run_pieces(GP - j)
                            if j == GP - 1:
                                while pend:
                                    pend.pop(0)()
                                if g == ngroups - 1 and pi_ == 0:
                                    S.op("pool", lambda e, gi=gi: e.tensor_copy(out=hTh, in_=hTg[gi][:, :, (GP - 1) * 128:GP * 128]),
                                         reads=(hTg_b[gi],), writes=(hTh_b,))
                                pend.extend(heads_pieces(g))
                                if g >= g_first + 1 and g + 1 < ngroups:
                                    load_tables(g + 1)
                        it["after"] = aft
                        items.append(it)

                def load_tables(g):
                    gi = gi_of[g]
                    S.load(cpt_b[gi], cpt[gi], cosP[:, g * GW:(g + 1) * GW])
                    S.load(cpt_b[gi], spt[gi], sinP[:, g * GW:(g + 1) * GW])
                load_tables(g_first)
                if g_first + 1 < ngroups:
                    load_tables(g_first + 1)
                norm_seq(NSC, items)
                while pend:
                    pend.pop(0)()
            for h in range(RH):
                S.op("pool", lambda e, h=h: e.tensor_copy(out=Sbf[:, h], in_=Sst[:, h]), reads=(Sst_b[h],), writes=(Sbf_b[h],))
        else:
            S.op("pool", lambda e: e.memset(hTh, 0.0), writes=(hTh_b,))
        S.barrier()

        R1.clear(); R2.clear(); R3.clear()
        hT = R1.alloc([128, KD, NTS], BF16)
        hT_b = Buf("hT")
        mixT = R2.alloc([128, 16, NTS], BF16)
        mixT_b = Buf("mixT")
        NSC = NormScratch(R3, g1)
        items = [dict(P=128, src_dram=xown[b * 128:(b + 1) * 128, :],
                      dst_fn=(lambda k0, k1, b=b: hT[:, k0:k1, b * 128:(b + 1) * 128]), dst_bufs=[hT_b]) for b in range(NB)]
        items.append(dict(P=NS, src_dram=xsmp[:, :], dst_fn=(lambda k0, k1: hT[:, k0:k1, NT:NTS]), dst_bufs=[hT_b]))
        norm_seq(NSC, items)
        S.barrier()

        def blk_cols(b):
            return (slice(b * 128, (b + 1) * 128), 128) if b < NB else (slice(NT, NTS), NS)

        R3.clear()
        A = R3
        msk, msk_b = const(A, [128, 2, 4, 128], BF16, masks, "msk")
        qtok_s = A.alloc([128, 1024], BF16)
        qtok_sb = Buf("qtoks")
        ks32 = A.alloc([128, 128], F32)
        vs32 = A.alloc([128, 128], F32)
        ksv_b = Buf("ksv")
        a2_mark = A.mark()
        qtk = [A.alloc([128, 256], BF16) for _ in range(4)]
        qtk_b = [Buf(f"qtk{i}") for i in range(4)]
        qT = A.alloc([128, 8, NT], BF16)
        qT_b = Buf("qT")
        Klo = A.alloc([128, 2, NKB * 128], BF16)
        Khi = A.alloc([128, 2, NKB * 128], BF16)
        K_b = Buf("K")
        vtok = A.alloc([128, NKB, 2, 64], BF16)
        vtok_b = Buf("vtok")
        kdup = A.alloc([128, 2, 2, 2, 64], BF16)
        kdup_b = [Buf("kdup0"), Buf("kdup1")]
        kn32 = A.alloc([128, 128], F32)
        v32 = A.alloc([128, 128], F32)
        kv32_b = Buf("kv32")
        NQ = 4
        sq = [A.alloc([128, 256], F32) for _ in range(NQ)]
        sq_b = [Buf(f"sq{i}") for i in range(NQ)]
        tmpn = [A.alloc([128, 256], F32) for _ in range(NQ)]
        tmpn_b = [Buf(f"tmpn{i}") for i in range(NQ)]
        pT = [A.alloc([128, 512], BF16) for _ in range(3)]
        pT_b = [Buf(f"pT{i}") for i in range(3)]
        aout = A.alloc([128, 1024], BF16)
        aout_b = Buf("aout")
        den = A.alloc([128, 16], F32)
        den_b = Buf("den")

        S.op("pool", lambda e: e.memset(Klo, 0.0), writes=(K_b,))
        S.op("pool", lambda e: e.memset(Khi, 0.0), writes=(K_b,))

        pslot = [0]

        def proj_tok(wt, wb_, src_hT, src_b, cols, P):
            s = pslot[0] % 8
            pslot[0] += 1
            bk, half = (0, 1, 3, 4)[s // 2], s % 2
            out_ap = banks[bk][:P, half * 256:(half + 1) * 256]
            kk = wt.shape[1]
            mm_group(bk, out_ap, [(src_hT[:, k, cols], wt[:, k, :]) for k in range(kk)], reads=(wb_, src_b))
            return out_ap, bk

        def headnorm(ps, bk, P, nh, i2):
            st_, st_bf = new_stat()
            S.op("act", lambda e: e.activation(out=sq[i2][:P, 0:nh * 64], in_=ps[:, 0:nh * 64], func=AF.Square),
                 reads=(PB_[bk],), writes=(sq_b[i2],))
            S.op("dve", lambda e: e.tensor_reduce(out=st_[:P, 0:nh], in_=sq[i2][:P, 0:nh * 64].rearrange("p (h d) -> p h d", h=nh),
                                                  axis=AX.X, op=ALU.add), reads=(sq_b[i2],), writes=(st_bf,))
            rstd_from_ssq(st_[:P, 0:nh], st_bf, P, 64, ncols=nh)
            S.op("dve", lambda e: e.tensor_tensor(
                out=tmpn[i2][:P, 0:nh * 64].rearrange("p (h d) -> p h d", h=nh),
                in0=ps[:, 0:nh * 64].rearrange("p (h d) -> p h d", h=nh),
                in1=st_[:P, 0:nh].unsqueeze(2).broadcast_to([P, nh, 64]), op=ALU.mult),
                reads=(PB_[bk], st_bf), writes=(tmpn_b[i2],))
            return tmpn[i2], tmpn_b[i2]

        def skewed(n_items, stage_fns, enable=True):
            k = len(stage_fns)
            if not enable:
                for i in range(n_items):
                    for f in stage_fns:
                        f(i)
                return
            for t in range(n_items + k - 1):
                for j in range(k):
                    i = t - j
                    if 0 <= i < n_items:
                        stage_fns[j](i)

        pitems = [("q", t, b) for t in range(4) for b in range(NB + 1)] + [("kv", 0, kb) for kb in range(NKB + 1)]
        pctx = {}
        wcur = {}

        def pj_s0(i):
            kind, t, b = pitems[i]
            if (kind, t) not in wcur:
                wcur[(kind, t)] = w_next()
            wt, wb_ = wcur[(kind, t)]
            if kind == "q":
                src_, src_b = hT, hT_b
                cols, P = blk_cols(b)
                nh = 4
            else:
                kb = b
                if kb == 0:
                    src_, src_b, cols, P = hTh, hTh_b, slice(0, 128), 128
                else:
                    src_, src_b = hT, hT_b
                    cols, P = blk_cols(kb - 1) if kb <= NB else blk_cols(NB)
                nh = 2
            ps, bk = proj_tok(wt, wb_, src_, src_b, cols, P)
            i2 = i % NQ
            st_, st_bf = new_stat()
            S.op("act", lambda e: e.activation(out=sq[i2][:P, 0:nh * 64], in_=ps[:, 0:nh * 64], func=AF.Square),
                 reads=(PB_[bk],), writes=(sq_b[i2],))
            pctx[i] = dict(ps=ps, bk=bk, P=P, nh=nh, i2=i2, st=st_, st_b=st_bf)

        def pj_s1(i):
            c = pctx[i]
            P, nh, i2, st_, st_bf = c["P"], c["nh"], c["i2"], c["st"], c["st_b"]
            S.op("dve", lambda e: e.tensor_reduce(out=st_[:P, 0:nh], in_=sq[i2][:P, 0:nh * 64].rearrange("p (h d) -> p h d", h=nh),
                                                  axis=AX.X, op=ALU.add), reads=(sq_b[i2],), writes=(st_bf,))
            rstd_from_ssq(st_[:P, 0:nh], st_bf, P, 64, ncols=nh)

        def pj_s2(i):
            kind, t, b = pitems[i]
            c = pctx[i]
            ps, bk, P, nh, i2, st_, st_bf = c["ps"], c["bk"], c["P"], c["nh"], c["i2"], c["st"], c["st_b"]
            tn, tn_b = tmpn[i2], tmpn_b[i2]
            S.op("dve", lambda e: e.tensor_tensor(
                out=tn[:P, 0:nh * 64].rearrange("p (h d) -> p h d", h=nh),
                in0=ps[:, 0:nh * 64].rearrange("p (h d) -> p h d", h=nh),
                in1=st_[:P, 0:nh].unsqueeze(2).broadcast_to([P, nh, 64]), op=ALU.mult),
                reads=(PB_[bk], st_bf), writes=(tn_b,))
            if kind == "q":
                if b < NB:
                    dq, dq_b = qtk[i2], qtk_b[i2]
                    dqa = dq[:P, :]
                else:
                    dq, dq_b = qtok_s, qtok_sb
                    dqa = qtok_s[:P, t * 256:(t + 1) * 256]
                S.op("dve", lambda e: e.tensor_tensor(
                    out=dqa.rearrange("p (h d) -> p h d", h=4),
                    in0=tn[:P, 0:256].rearrange("p (h d) -> p h d", h=4),
                    in1=gqb[:P].unsqueeze(1).broadcast_to([P, 4, 64]), op=ALU.mult),
                    reads=(tn_b, gqb_b), writes=(dq_b,))
                c["dq"], c["dq_b"] = dq, dq_b
            else:
                kb = b
                if kb <= NB:
                    ks_ = kb % 2
                    S.op("dve", lambda e: e.tensor_tensor(
                        out=kdup[:, ks_], in0=tn[:, 0:128].rearrange("p (g d) -> p g d", g=2).unsqueeze(2).broadcast_to([128, 2, 2, 64]),
                        in1=gkb[:].unsqueeze(1).unsqueeze(1).broadcast_to([128, 2, 2, 64]), op=ALU.mult),
                        reads=(tn_b, gkb_b), writes=(kdup_b[ks_],))
                    S.op("act", lambda e: e.activation(
                        out=vtok[:, kb], in_=ps[:, 128:256].rearrange("p (g d) -> p g d", g=2), func=AF.Copy),
                        reads=(PB_[bk],), writes=(vtok_b,))
                    if kb == NB:
                        S.op("dve", lambda e: e.tensor_tensor(
                            out=kn32[:].rearrange("p (g d) -> p g d", g=2), in0=tn[:, 0:128].rearrange("p (g d) -> p g d", g=2),
                            in1=gkb[:].unsqueeze(1).broadcast_to([128, 2, 64]), op=ALU.mult),
                            reads=(tn_b, gkb_b), writes=(kv32_b,))
                        S.op("act", lambda e: e.activation(out=v32, in_=ps[:, 128:256], func=AF.Copy),
                             reads=(PB_[bk],), writes=(kv32_b,))
                        S.store(kv32_b, kwin_o, kn32[:])
                        S.store(kv32_b, vwin_o, v32[:])
                else:
                    S.op("dve", lambda e: e.tensor_tensor(
                        out=ks32[:NS].rearrange("p (g d) -> p g d", g=2), in0=tn[:NS, 0:128].rearrange("p (g d) -> p g d", g=2),
                        in1=gkb[:NS].unsqueeze(1).broadcast_to([NS, 2, 64]), op=ALU.mult),
                        reads=(tn_b, gkb_b), writes=(ksv_b,))
                    S.op("act", lambda e: e.activation(out=vs32[:NS], in_=ps[:, 128:256], func=AF.Copy),
                         reads=(PB_[bk],), writes=(ksv_b,))

        def pj_s3(i):
            kind, t, b = pitems[i]
            c = pctx.pop(i)
            if kind == "q":
                if b < NB:
                    dq, dq_b = c["dq"], c["dq_b"]
                    pv = bank_bf(2)[:, (i % 2) * 256:(i % 2) * 256 + 256].rearrange("p (c t) -> p c t", c=2)

                    def trq(e):
                        ins = None
                        for cc in range(2):
                            ins = e.transpose(out=pv[:, cc, :], in_=dq[:, cc * 128:(cc + 1) * 128], identity=ident)
                        return ins
                    S.op("pe", trq, reads=(dq_b, ident_b), writes=(PB_[2],))
                    S.op("act", lambda e: e.activation(out=qT[:, 2 * t:2 * t + 2, b * 128:(b + 1) * 128], in_=pv,
                                                       func=AF.Copy), reads=(PB_[2],), writes=(qT_b,))
            else:
                kb = b
                if kb <= NB:
                    ks_ = kb % 2
                    pv = bank_bf(2)[:, 512 + ks_ * 256:512 + ks_ * 256 + 256].rearrange("p (g t) -> p g t", g=2)

                    def trd(e):
                        ins = None
                        for g_ in range(2):
                            ins = e.transpose(out=pv[:, g_, :], in_=kdup[:, ks_, g_].rearrange("p a d -> p (a d)"), identity=ident)
                        return ins
                    S.op("pe", trd, reads=(kdup_b[ks_], ident_b), writes=(PB_[2],))
                    S.op("dve", lambda e: e.tensor_copy(out=Klo[0:64, :, kb * 128:(kb + 1) * 128], in_=pv[0:64]),
                         reads=(PB_[2],), writes=(K_b,))
                    S.op("dve", lambda e: e.tensor_copy(out=Khi[64:128, :, kb * 128:(kb + 1) * 128], in_=pv[64:128]),
                         reads=(PB_[2],), writes=(K_b,))
        if cfg.get("PJ_SKEW2", True):
            def pj_s12(i):
                pj_s1(i)
                pj_s2(i)
            if cfg.get("PJ_SKEW3", True):
                skewed(len(pitems), [pj_s0, pj_s12, pj_s3], True)
            else:
                def pj_s012(i):
                    pj_s0(i)
                    pj_s12(i)
                skewed(len(pitems), [pj_s012, pj_s3], True)
        else:
            skewed(len(pitems), [pj_s0, pj_s1, pj_s2, pj_s3], cfg.get("PJ_SKEW", False))

        aitems = [(n, c) for n in range(NB) for c in range(8)]

        def at_s0(i):
            n, c = aitems[i]
            own = slice((n + 1) * 128, (n + 2) * 128)
            prv = slice(n * 128, (n + 1) * 128)
            qc = slice(n * 128, (n + 1) * 128)
            mk = msk[:, 0 if n == 0 else 1].rearrange("p a t -> p (a t)")
            g_ = c // 4
            sb = 3 + i % 2
            pi = i % 3

            def scf(e):
                ins = None
                for j, (Kt, ks) in enumerate(((Klo, own), (Klo, prv), (Khi, own), (Khi, prv))):
                    ins = e.matmul(banks[sb][:, j * 128:(j + 1) * 128], lhsT=Kt[:, g_, ks], rhs=qT[:, c, qc],
                                   start=True, stop=True)
                return ins
            S.op("pe", scf, reads=(K_b, qT_b), writes=(PB_[sb],))
            S.op("act", lambda e: e.activation(out=pT[pi], in_=banks[sb][:, :], func=AF.Exp, scale=0.125),
                 reads=(PB_[sb],), writes=(pT_b[pi],))
            S.op("dve", lambda e: e.tensor_tensor(out=pT[pi], in0=pT[pi], in1=mk, op=ALU.mult),
                 reads=(pT_b[pi], msk_b), writes=(pT_b[pi],))

        def at_s1(i):
            n, c = aitems[i]
            g_ = c // 4
            pi = i % 3
            qc = slice(n * 128, (n + 1) * 128)

            def pvf(e):
                ins = None
                for par in range(2):
                    h = 2 * c + par
                    ob = 5 + h // 8
                    oo = banks[ob][:, (h % 8) * 64:(h % 8 + 1) * 64]
                    e.matmul(oo, lhsT=pT[pi][:, (2 * par) * 128:(2 * par + 1) * 128], rhs=vtok[:, n + 1, g_, :],
                             start=True, stop=False)
                    e.matmul(oo, lhsT=pT[pi][:, (2 * par + 1) * 128:(2 * par + 2) * 128], rhs=vtok[:, n, g_, :],
                             start=False, stop=True)
                    rr = banks[7][:, h:h + 1]
                    e.matmul(rr, lhsT=pT[pi][:, (2 * par) * 128:(2 * par + 1) * 128], rhs=ones_bf[:, 0:1],
                             start=True, stop=False)
                    ins = e.matmul(rr, lhsT=pT[pi][:, (2 * par + 1) * 128:(2 * par + 2) * 128], rhs=ones_bf[:, 0:1],
                                   start=False, stop=True)
                return ins
            S.op("pe", pvf, reads=(pT_b[pi], vtok_b, cst_b), writes=(PB_[5], PB_[6], PB_[7]))
            if c == 7:
                S.op("dve", lambda e: e.tensor_tensor(out=den, in0=banks[7][:, 0:16], in1=esnk, op=ALU.add),
                     reads=(PB_[7], esnk_b), writes=(den_b,))
                S.op("dve", lambda e: e.reciprocal(out=den, in_=den), reads=(den_b,), writes=(den_b,))
                for hb in range(2):
                    S.op("dve", lambda e, hb=hb: e.tensor_tensor(
                        out=aout[:, hb * 512:(hb + 1) * 512].rearrange("p (h d) -> p h d", h=8),
                        in0=banks[5 + hb][:, :].rearrange("p (h d) -> p h d", h=8),
                        in1=den[:, hb * 8:(hb + 1) * 8].unsqueeze(2).broadcast_to([128, 8, 64]), op=ALU.mult),
                        reads=(PB_[5 + hb], den_b), writes=(aout_b,))
                pv = bank_bf(2)[:, 0:1024].rearrange("p (c t) -> p c t", c=8)

                def tra(e):
                    ins = None
                    for cc in range(8):
                        ins = e.transpose(out=pv[:, cc, :], in_=aout[:, cc * 128:(cc + 1) * 128], identity=ident)
                    return ins
                S.op("pe", tra, reads=(aout_b, ident_b), writes=(PB_[2],))
                S.op("act", lambda e: e.activation(out=mixT[:, 0:8, qc], in_=pv, func=AF.Copy),
                     reads=(PB_[2],), writes=(mixT_b,))
        skewed(len(aitems), [at_s0, at_s1], cfg.get("AT_SKEW", True))
        S.barrier()

        A.reset(a2_mark)
        sel, sel_b = const(A, [NS, NS, 128], BF16, sel_d, "sel")
        aout2 = A.alloc([128, 1024], BF16)
        aout2_b = Buf("aout2")
        den2 = A.alloc([128, 16], F32)
        den2_b = Buf("den2")
        ck_sb = A.alloc([128, NS, 128], F32)
        cv_sb = A.alloc([128, NS, 128], F32)
        cv_bf = A.alloc([128, NS, 128], BF16)
        ckv_b = Buf("ckv")
        cvbf_b = Buf("cvbf")
        S.load(ckv_b, ck_sb, ck.rearrange("b j f -> j b f"))
        S.load(ckv_b, cv_sb, cv.rearrange("b j f -> j b f"))
        S.op("pool", lambda e: e.tensor_copy(out=cv_bf, in_=cv_sb), reads=(ckv_b,), writes=(cvbf_b,))
        S.dma(ksw_o[:, 0:127, :], ck[:, 1:128, :], final=True)
        S.dma(vsw_o[:, 0:127, :], cv[:, 1:128, :], final=True)
        S.store(ksv_b, ksw_o[:, 127, :], ks32[:NS])
        S.store(ksv_b, vsw_o[:, 127, :], vs32[:NS])

        prod = A.alloc([128, 1024], F32)
        prod_b = Buf("prod")
        sTs = A.alloc([128, NS, 16], F32)
        sTs_b = Buf("sTs")
        pTs = A.alloc([128, NS, 16], BF16)
        pTs_b = Buf("pTs")
        Pm = A.alloc([128, 16, NS, NS], BF16)
        Pm_b = Buf("Pm")
        for b in range(NS):
            for hb in range(2):
                qb = (3 + hb) if b % 2 == 0 else (5 + hb)
                mm_group(qb, banks[qb][:, :], [(sel[:NS, b, :], qtok_s[:NS, hb * 512:(hb + 1) * 512])],
                         reads=(sel_b, qtok_sb))
                S.op("dve", lambda e, b=b, hb=hb, qb=qb: e.tensor_tensor(
                    out=prod[:, hb * 512:(hb + 1) * 512].rearrange("p (h d) -> p h d", h=8),
                    in0=banks[qb][:, :].rearrange("p (h d) -> p h d", h=8),
                    in1=ck_sb[:, b, hb * 64:(hb + 1) * 64].unsqueeze(1).broadcast_to([128, 8, 64]), op=ALU.mult),
                    reads=(PB_[qb], ckv_b), writes=(prod_b,))
            S.op("dve", lambda e, b=b: e.tensor_reduce(out=sTs[:, b, :], in_=prod[:].rearrange("p (h d) -> p h d", h=16),
                                                       axis=AX.X, op=ALU.add), reads=(prod_b,), writes=(sTs_b,))
        S.op("act", lambda e: e.activation(out=pTs, in_=sTs, func=AF.Exp, scale=0.125), reads=(sTs_b,), writes=(pTs_b,))
        for h in range(16):
            S.op("dve", lambda e, h=h: e.tensor_tensor(
                out=Pm[:, h], in0=pTs[:, :, h:h + 1].broadcast_to([128, NS, NS]), in1=eye, op=ALU.mult),
                reads=(pTs_b, eye_b), writes=(Pm_b,))

        def spv(e):
            ins = None
            for h in range(16):
                g_ = h // 8
                ob = 5 + h // 8
                for b in range(NS):
                    e.matmul(banks[ob][:NS, (h % 8) * 64:(h % 8 + 1) * 64], lhsT=Pm[:, h, b, :],
                             rhs=cv_bf[:, b, g_ * 64:(g_ + 1) * 64], start=(b == 0), stop=(b == NS - 1))
                for b in range(NS):
                    ins = e.matmul(banks[7][:NS, h:h + 1], lhsT=Pm[:, h, b, :], rhs=ones_bf[:, 0:1],
                                   start=(b == 0), stop=(b == NS - 1))
            return ins
        S.op("pe", spv, reads=(Pm_b, cvbf_b, cst_b), writes=(PB_[5], PB_[6], PB_[7]))
        snew = A.alloc([128, 16], F32)
        pnew = A.alloc([128, 16], F32)
        snew_b = Buf("snew")
        num = A.alloc([128, 1024], F32)
        num_b = Buf("num")
        S.op("dve", lambda e: e.tensor_tensor(
            out=prod[:NS].rearrange("p (g h d) -> p g h d", g=2, h=8),
            in0=qtok_s[:NS, :].rearrange("p (g h d) -> p g h d", g=2, h=8),
            in1=ks32[:NS].rearrange("p (g d) -> p g d", g=2).unsqueeze(2).broadcast_to([NS, 2, 8, 64]), op=ALU.mult),
            reads=(qtok_sb, ksv_b), writes=(prod_b,))
        S.op("dve", lambda e: e.tensor_reduce(out=snew[:NS], in_=prod[:NS].rearrange("p (h d) -> p h d", h=16),
                                              axis=AX.X, op=ALU.add), reads=(prod_b,), writes=(snew_b,))
        S.op("act", lambda e: e.activation(out=pnew[:NS], in_=snew[:NS], func=AF.Exp, scale=0.125),
             reads=(snew_b,), writes=(snew_b,))
        S.op("dve", lambda e: e.tensor_tensor(
            out=num[:NS].rearrange("p (g h d) -> p g h d", g=2, h=8),
            in0=vs32[:NS].rearrange("p (g d) -> p g d", g=2).unsqueeze(2).broadcast_to([NS, 2, 8, 64]),
            in1=pnew[:NS].rearrange("p (g h) -> p g h", g=2).unsqueeze(3).broadcast_to([NS, 2, 8, 64]), op=ALU.mult),
            reads=(ksv_b, snew_b), writes=(num_b,))
        for hb in range(2):
            S.op("dve", lambda e, hb=hb: e.tensor_tensor(out=num[:NS, hb * 512:(hb + 1) * 512], in0=banks[5 + hb][:NS, :],
                                                         in1=num[:NS, hb * 512:(hb + 1) * 512], op=ALU.add),
                 reads=(PB_[5 + hb], num_b), writes=(num_b,))
        S.op("dve", lambda e: e.tensor_tensor(out=den2[:NS], in0=banks[7][:NS, 0:16], in1=esnk[:NS], op=ALU.add),
             reads=(PB_[7], esnk_b), writes=(den2_b,))
        S.op("dve", lambda e: e.tensor_tensor(out=den2[:NS], in0=den2[:NS], in1=pnew[:NS], op=ALU.add),
             reads=(den2_b, snew_b), writes=(den2_b,))
        S.op("dve", lambda e: e.reciprocal(out=den2[:NS], in_=den2[:NS]), reads=(den2_b,), writes=(den2_b,))
        S.op("dve", lambda e: e.tensor_tensor(
            out=aout2[:NS].rearrange("p (h d) -> p h d", h=16), in0=num[:NS].rearrange("p (h d) -> p h d", h=16),
            in1=den2[:NS].unsqueeze(2).broadcast_to([NS, 16, 64]), op=ALU.mult), reads=(num_b, den2_b), writes=(aout2_b,))
        pvs = bank_bf(2)[:, 0:8 * NS].rearrange("p (c t) -> p c t", c=8)

        def tras(e):
            ins = None
            for c in range(8):
                ins = e.transpose(out=pvs[:, c, :], in_=aout2[:NS, c * 128:(c + 1) * 128], identity=ident[:NS, :NS])
            return ins
        S.op("pe", tras, reads=(aout2_b, ident_b), writes=(PB_[2],))
        S.op("act", lambda e: e.activation(out=mixT[:, 0:8, NT:NTS], in_=pvs, func=AF.Copy), reads=(PB_[2],), writes=(mixT_b,))
        S.barrier()

        R3.clear()
        A = R3
        cT = A.alloc([128, NT], F32)
        sT_ = A.alloc([128, NT], F32)
        cs_b = Buf("cs")
        S.load(cs_b, cT, cosT)
        S.load(cs_b, sT_, sinT)
        rt = [A.alloc([128, 512], F32) for _ in range(4)]
        rt_b = [Buf(f"rtB{i}") for i in range(4)]
        qTh = A.alloc([128, 2, NT], BF16)
        kTh = A.alloc([128, 2, NT], BF16)
        qdT = A.alloc([128, 2, NT], BF16)
        qTh_b, kTh_b, qdT_b = Buf("qTh"), Buf("kTh"), Buf("qdT")
        ktok = A.alloc([128, NB, 256], BF16)
        ktok_b = Buf("ktok")
        rv = A.alloc([128, NB, 256], BF16)
        rv_b = Buf("rv")
        gg = A.alloc([128, NB + 1, 256], BF16)
        gg_b = Buf("gg")
        gate32 = [A.alloc([128, 256], F32) for _ in range(2)]
        gate32_b = [Buf("gate0"), Buf("gate1")]
        attm = [A.alloc([128, 128], BF16) for _ in range(2)]
        attm_b = [Buf("attm0"), Buf("attm1")]
        rout = [A.alloc([128, 256], BF16) for _ in range(3)]
        rout_b = [Buf("rout0"), Buf("rout1"), Buf("rout2")]
        SbfB = A.alloc([128, RH, 2, 256], BF16)
        SbfB_b = [Buf(f"SbfB{h}") for h in range(RH)]
        qs32 = A.alloc([128, 256], F32)
        ks32r = A.alloc([128, 256], F32)
        qs_rot = A.alloc([128, 256], F32)
        ks_rot = A.alloc([128, 256], F32)
        qs_bf = A.alloc([128, 256], BF16)
        rvs32 = A.alloc([128, 256], F32)
        rvs_bf = A.alloc([128, 256], BF16)
        smp_b = Buf("smp")
        rtS = [A.alloc([128, 128], F32) for _ in range(4)]
        rtS_b = [Buf(f"rtS{i}") for i in range(4)]
        qsT = A.alloc([128, 2, NS], BF16)
        qsT_b = Buf("qsT")
        Qm = A.alloc([128, 2, NS, NS], BF16)
        Qm_b = Buf("Qm")
        Km = [A.alloc([128, 256], BF16) for _ in range(2)]
        Km_b = [Buf("Km0"), Buf("Km1")]
        s32_b = Buf("s32")
        qThB = A.alloc([128, 2, NT], BF16)
        kThB = A.alloc([128, 2, NT], BF16)
        qdTB = A.alloc([128, 2, NT], BF16)
        qThB_b, kThB_b, qdTB_b = Buf("qThB"), Buf("kThB"), Buf("qdTB")
        NSL = 3
        stt = [A.alloc([128, 2, 256], F32) for _ in range(NSL)]
        stt_b = [Buf(f"stt{i}") for i in range(NSL)]
        sbf = [A.alloc([128, 2, 256], BF16) for _ in range(2)]
        sbf_b = [Buf("sbf0"), Buf("sbf1")]
        qk = A.alloc([128, 8], F32)
        qk_b = Buf("qk")
        os32 = A.alloc([128, 256], F32)
        os_b = Buf("os")
        stt_i = [0]
        r_i = [0]

        def rot_tok(x32, xb, out):
            x1, x2 = x32[:NS, 0:128], x32[:NS, 128:256]
            t1, t2, t3, t4 = [r[:NS] for r in rtS]
            S.op("dve", lambda e: e.tensor_tensor(out=t1, in0=x1, in1=cSs, op=ALU.mult), reads=(xb, cSs_b), writes=(rtS_b[0],))
            S.op("dve", lambda e: e.tensor_tensor(out=t2, in0=x2, in1=sSs, op=ALU.mult), reads=(xb, sSs_b), writes=(rtS_b[1],))
            S.op("dve", lambda e: e.tensor_tensor(out=t3, in0=x2, in1=cSs, op=ALU.mult), reads=(xb, cSs_b), writes=(rtS_b[2],))
            S.op("dve", lambda e: e.tensor_tensor(out=t4, in0=x1, in1=sSs, op=ALU.mult), reads=(xb, sSs_b), writes=(rtS_b[3],))
            S.op("dve", lambda e: e.tensor_tensor(out=out[:NS, 0:128], in0=t1, in1=t2, op=ALU.subtract),
                 reads=(rtS_b[0], rtS_b[1]), writes=(smp_b,))
            S.op("dve", lambda e: e.tensor_tensor(out=out[:NS, 128:256], in0=t3, in1=t4, op=ALU.add),
                 reads=(rtS_b[2], rtS_b[3]), writes=(smp_b,))

        def rng_A(o_ps, o_bk, P, ggrow):
            ri = r_i[0] % 3
            r_i[0] += 1
            st_, st_bf = new_stat()
            S.op("act", lambda e: e.activation(out=junk[:P, 0:256], in_=o_ps, func=AF.Square, accum_out=st_[:P, 0:1]),
                 reads=(PB_[o_bk],), writes=(junk_b, st_bf))
            rstd_from_ssq(st_[:P, 0:1], st_bf, P, 256)
            S.op("dve", lambda e: e.scalar_tensor_tensor(out=rout[ri][:P], in0=o_ps, scalar=st_[:P, 0:1], in1=ggrow,
                                                         op0=ALU.mult, op1=ALU.mult),
                 reads=(PB_[o_bk], st_bf, gg_b), writes=(rout_b[ri],))
            return ri

        def ret_norm_gate(o_ps, o_bk, P, ggrow, cols, h):
            rng_B(rng_A(o_ps, o_bk, P, ggrow), P, cols, h)

        def rng_B(ri, P, cols, h):
            pv = bank_bf(3)[:, 0:2 * P].rearrange("p (c t) -> p c t", c=2)

            def trr(e, ri=ri, pv=pv):
                ins = None
                for c in range(2):
                    ins = e.transpose(out=pv[:, c, :], in_=rout[ri][:P, c * 128:(c + 1) * 128], identity=ident[:P, :P])
                return ins
            S.op("pe", trr, reads=(rout_b[ri], ident_b), writes=(PB_[3],))
            S.op("act", lambda e, pv=pv: e.activation(out=mixT[:, 8 + 2 * h:10 + 2 * h, cols], in_=pv, func=AF.Copy),
                 reads=(PB_[3],), writes=(mixT_b,))

        tgroups = [(t0, min(512, NT - t0)) for t0 in range(0, NT, 512)]
        qTh2, kTh2, qdT2 = [qTh, qThB], [kTh, kThB], [qdT, qdTB]
        qTh2_b, kTh2_b, qdT2_b = [qTh_b, qThB_b], [kTh_b, kThB_b], [qdT_b, qdTB_b]

        def qk_chunks(h, par):
            chunks = []
            wts = {}
            for wi, (dstT, dst_b, s32) in enumerate(((qTh2[par], qTh2_b[par], qs32), (kTh2[par], kTh2_b[par], ks32r))):
                for ti, (t0, tn_) in enumerate(tgroups):
                    def ch(wi=wi, dstT=dstT, dst_b=dst_b, s32=s32, ti=ti, t0=t0, tn_=tn_):
                        if ti == 0:
                            wts[wi] = w_next()
                        wt, wb_ = wts[wi]
                        for dc in range(2):
                            mm_group(dc, banks[dc][:, 0:tn_],
                                     [(wt[:, k, dc * 128:(dc + 1) * 128], hT[:, k, t0:t0 + tn_]) for k in range(KD)],
                                     reads=(wb_, hT_b))
                        def post():
                            rotary(banks[0][:, 0:tn_], banks[1][:, 0:tn_], (PB_[0], PB_[1]), cT[:, t0:t0 + tn_], sT_[:, t0:t0 + tn_],
                                   (cs_b,), dstT[:, 0, t0:t0 + tn_], dstT[:, 1, t0:t0 + tn_], (dst_b,), [r[:, 0:tn_] for r in rt], rt_b)
                            if ti == len(tgroups) - 1:
                                mm_group(2, banks[2][:NS, 0:256], [(hT[:, k, NT:NTS], wt[:, k, :]) for k in range(KD)], reads=(wb_, hT_b))
                                S.op("act", lambda e: e.activation(out=s32[:NS], in_=banks[2][:NS, 0:256], func=AF.Copy,
                                                                   scale=(1.0 if wi == 0 else 1.0 / 16.0)),
                                     reads=(PB_[2],), writes=(s32_b,))
                                if wi == 0:
                                    S.op("pool", lambda e: e.tensor_tensor(
                                        out=qdT2[par][:].rearrange("p c (n i) -> p c n i", i=128),
                                        in0=qTh2[par][:].rearrange("p c (n i) -> p c n i", i=128),
                                        in1=qdc[:, h, :].unsqueeze(1).unsqueeze(1).broadcast_to([128, 2, NB, 128]), op=ALU.mult),
                                        reads=(qTh2_b[par], qdc_b), writes=(qdT2_b[par],))
                        return post
                    chunks.append(ch)
            return chunks

        def head_body(h, par, next_chunks):
            qTh, kTh, qdT = qTh2[par], kTh2[par], qdT2[par]
            qTh_b, kTh_b, qdT_b = qTh2_b[par], kTh2_b[par], qdT2_b[par]
            rot_tok(qs32, s32_b, qs_rot)
            rot_tok(ks32r, s32_b, ks_rot)
            S.op("pool", lambda e: e.tensor_copy(out=qs_bf[:NS], in_=qs_rot[:NS]), reads=(smp_b,), writes=(smp_b,))
            for n0 in range(0, NB, 4):
                nn_ = min(4, NB - n0)
                pv = bank_bf(3)[:, 0:nn_ * 256].rearrange("p (j c) -> p j c", j=nn_)

                def trk2(e, n0=n0, nn_=nn_, pv=pv):
                    ins = None
                    for j in range(nn_):
                        for dc in range(2):
                            ins = e.transpose(out=pv[:, j, dc * 128:(dc + 1) * 128],
                                              in_=kTh[:, dc, (n0 + j) * 128:(n0 + j + 1) * 128], identity=ident)
                    return ins
                S.op("pe", trk2, reads=(kTh_b, ident_b), writes=(PB_[3],))
                S.op("act", lambda e, pv=pv, n0=n0, nn_=nn_, h=h: e.activation(out=ktok[:, n0:n0 + nn_, :], in_=pv, func=AF.Copy,
                                                                                scale=kdc[:, h:h + 1]),
                     reads=(PB_[3], kdc_b), writes=(ktok_b,))
            wt, wb_ = w_next()
            for b in range(NB + 1):
                cols, P = blk_cols(b)
                half = b % 2
                mm_group(2, banks[2][:P, half * 256:(half + 1) * 256], [(hT[:, k, cols], wt[:, k, :]) for k in range(KD)],
                         reads=(wb_, hT_b))
                if b < NB:
                    S.op("act", lambda e, b=b, half=half: e.activation(out=rv[:, b, :], in_=banks[2][:, half * 256:(half + 1) * 256],
                                                                       func=AF.Copy), reads=(PB_[2],), writes=(rv_b,))
                else:
                    S.op("act", lambda e, half=half: e.activation(out=rvs32[:NS], in_=banks[2][:NS, half * 256:(half + 1) * 256],
                                                                  func=AF.Copy), reads=(PB_[2],), writes=(smp_b,))
                    S.op("pool", lambda e: e.tensor_copy(out=rvs_bf[:NS], in_=rvs32[:NS]), reads=(smp_b,), writes=(smp_b,))
            wt, wb_ = w_next()
            for b in range(NB + 1):
                cols, P = blk_cols(b)
                half = b % 2
                gi = b % 2
                mm_group(2, banks[2][:P, half * 256:(half + 1) * 256], [(hT[:, k, cols], wt[:, k, :]) for k in range(KD)],
                         reads=(wb_, hT_b))
                S.op("act", lambda e, P=P, half=half, gi=gi: e.activation(out=gate32[gi][:P], in_=banks[2][:P, half * 256:(half + 1) * 256],
                                                                          func=AF.Silu), reads=(PB_[2],), writes=(gate32_b[gi],))
                S.op("pool", lambda e, P=P, b=b, gi=gi, h=h: e.tensor_tensor(out=gg[:P, b, :], in0=gate32[gi][:P],
                                                                             in1=gretb[:P, h * 256:(h + 1) * 256], op=ALU.mult),
                     reads=(gate32_b[gi], gretb_b), writes=(gg_b,))
            pvq = bank_bf(3)[:, 0:2 * NS].rearrange("p (c t) -> p c t", c=2)

            def trqs(e, pvq=pvq):
                ins = None
                for c in range(2):
                    ins = e.transpose(out=pvq[:, c, :], in_=qs_bf[:NS, c * 128:(c + 1) * 128], identity=ident[:NS, :NS])
                return ins
            S.op("pe", trqs, reads=(smp_b, ident_b), writes=(PB_[3],))
            S.op("act", lambda e, pvq=pvq: e.activation(out=qsT, in_=pvq, func=AF.Copy), reads=(PB_[3],), writes=(qsT_b,))
            for dc in range(2):
                S.op("dve", lambda e, dc=dc: e.tensor_tensor(
                    out=Qm[:, dc], in0=qsT[:, dc, :].unsqueeze(2).broadcast_to([128, NS, NS]), in1=eye, op=ALU.mult),
                    reads=(qsT_b, eye_b), writes=(Qm_b,))
            S.op("dve", lambda e: e.tensor_tensor(out=os32[:NS], in0=qs_rot[:NS], in1=ks_rot[:NS], op=ALU.mult),
                 reads=(smp_b,), writes=(os_b,))
            S.op("dve", lambda e: e.tensor_reduce(out=qk[:NS, 0:1], in_=os32[:NS], axis=AX.X, op=ALU.add),
                 reads=(os_b,), writes=(qk_b,))
            def smp_A(b, h=h):
                si = stt_i[0] % NSL
                bi = stt_i[0] % 2
                stt_i[0] += 1
                S.load(stt_b[si], stt[si], st0[b, h].rearrange("(c p) v -> p c v", p=128))
                S.op("act", lambda e: e.activation(out=sbf[bi], in_=stt[si], func=AF.Copy), reads=(stt_b[si],), writes=(sbf_b[bi],))
                S.op("dve", lambda e: e.tensor_scalar(out=Km[bi][:NS], in0=ks_rot[:NS], scalar1=eyeT[:NS, b:b + 1],
                                                      scalar2=None, op0=ALU.mult),
                     reads=(smp_b, eyeT_b), writes=(Km_b[bi],))
                return (si, bi)

            def smp_B(b, ctx, h=h):
                si, bi = ctx

                def qs0(e):
                    ins = None
                    for dc in range(2):
                        ins = e.matmul(banks[7][:NS, 0:256], lhsT=Qm[:, dc, b, :], rhs=sbf[bi][:, dc, :],
                                       start=(b == 0 and dc == 0), stop=(b == NS - 1 and dc == 1))
                    return ins
                S.op("pe", qs0, reads=(Qm_b, sbf_b[bi]), writes=(PB_[7],))

                def kvs(e):
                    ins = None
                    for dc in range(2):
                        ins = e.matmul(banks[2][:, dc * 256:(dc + 1) * 256], lhsT=Km[bi][:NS, dc * 128:(dc + 1) * 128],
                                       rhs=rvs_bf[:NS, :], start=True, stop=True)
                    return ins
                S.op("pe", kvs, reads=(Km_b[bi], smp_b), writes=(PB_[2],))
                S.op("dve", lambda e: e.scalar_tensor_tensor(
                    out=stt[si][:].rearrange("p a b -> p (a b)"), in0=stt[si][:].rearrange("p a b -> p (a b)"),
                    scalar=float(gam[h]), in1=banks[2][:, 0:512], op0=ALU.mult, op1=ALU.add),
                    reads=(PB_[2], stt_b[si]), writes=(stt_b[si],))
                S.store(stt_b[si], ssn_o[b, h].rearrange("(c p) v -> p c v", p=128), stt[si])

            SPB = NS // NB
            Sb_ap = [Sbf, SbfB]
            Sb_bf = [Sbf_b, SbfB_b]
            bctx = {}
            sctx = {}
            for b_ in range(min(2, NS)):
                sctx[b_] = smp_A(b_)

            def rb_t0(n, h=h):
                cs_ = slice(n * 128, (n + 1) * 128)
                ai = n % 2
                aslot = banks[4][:, (n % 4) * 128:(n % 4 + 1) * 128]
                mm_group(4, aslot, [(kTh[:, dc, cs_], qTh[:, dc, cs_]) for dc in range(2)], reads=(kTh_b, qTh_b))
                S.op("dve", lambda e: e.tensor_tensor(out=attm[ai], in0=aslot, in1=dmk[:, h, :], op=ALU.mult),
                     reads=(PB_[4], dmk_b), writes=(attm_b[ai],))

                def kvf2(e):
                    ins = None
                    for dc in range(2):
                        ins = e.matmul(banks[6][:, dc * 256:(dc + 1) * 256], lhsT=ktok[:, n, dc * 128:(dc + 1) * 128],
                                       rhs=rv[:, n, :], start=True, stop=True)
                    return ins
                S.op("pe", kvf2, reads=(ktok_b, rv_b), writes=(PB_[6],))
                S.op("dve", lambda e: e.scalar_tensor_tensor(
                    out=Sst[:, h].rearrange("p a b -> p (a b)"), in0=Sst[:, h].rearrange("p a b -> p (a b)"),
                    scalar=float(cdec[h]), in1=banks[6][:, 0:512], op0=ALU.mult, op1=ALU.add),
                    reads=(PB_[6], Sst_b[h]), writes=(Sst_b[h],))
                if n < NB - 1:
                    nxt = (n + 1) % 2
                    S.op("act", lambda e: e.activation(out=Sb_ap[nxt][:, h], in_=Sst[:, h], func=AF.Copy),
                         reads=(Sst_b[h],), writes=(Sb_bf[nxt][h],))

            def rb_t1(n, h=h):
                cs_ = slice(n * 128, (n + 1) * 128)
                ai = n % 2
                cur = n % 2
                oslot = banks[5][:, (n % 2) * 256:(n % 2 + 1) * 256]
                mm_group(5, oslot, [(attm[ai], rv[:, n, :])] + [(qdT[:, dc, cs_], Sb_ap[cur][:, h, dc, :]) for dc in range(2)],
                         reads=(attm_b[ai], rv_b, qdT_b, Sb_bf[cur][h]))
                bctx[n] = rng_A(oslot, 5, 128, gg[:, n, :])

            def rb_t2(n, h=h):
                cs_ = slice(n * 128, (n + 1) * 128)
                rng_B(bctx.pop(n), 128, cs_, h)
                for b_ in range(n * SPB, (n + 1) * SPB):
                    smp_B(b_, sctx.pop(b_))
                    if b_ + 2 < NS:
                        sctx[b_ + 2] = smp_A(b_ + 2)
            if cfg.get("RB_SKEW", True):
                for t_ in range(NB + 2):
                    if 0 <= t_ - 1 < NB:
                        rb_t1(t_ - 1)
                    if t_ < NB:
                        rb_t0(t_)
                    if 0 <= t_ - 2 < NB:
                        rb_t2(t_ - 2)
                    if cfg.get("QK_OVERLAP", True):
                        if t_ % 2 == 0 and posts:
                            posts.pop(0)()
                        if t_ % 2 == 1 and next_chunks:
                            posts.append(next_chunks.pop(0)())
            else:
                skewed(NB, [rb_t0, rb_t1, rb_t2], False)
            S.store(Sst_b[h], sret_o[h].rearrange("(c p) v -> p c v", p=128), Sst[:, h])
            S.op("dve", lambda e: e.tensor_scalar(out=os32[:NS], in0=rvs32[:NS], scalar1=qk[:NS, 0:1], scalar2=None, op0=ALU.mult),
                 reads=(smp_b, qk_b), writes=(os_b,))
            S.op("dve", lambda e, h=h: e.scalar_tensor_tensor(out=banks[7][:NS, 256:512], in0=banks[7][:NS, 0:256], scalar=float(gam[h]),
                                                              in1=os32[:NS], op0=ALU.mult, op1=ALU.add),
                 reads=(PB_[7], os_b), writes=(PB_[7],))
            ret_norm_gate(banks[7][:NS, 256:512], 7, NS, gg[:NS, NB, :], slice(NT, NTS), h)

        posts = []
        first_chunks = qk_chunks(0, 0)
        for ch_ in first_chunks:
            ch_()()
        for h in range(RH):
            nxt_chunks = qk_chunks(h + 1, (h + 1) % 2) if h + 1 < RH else []
            head_body(h, h % 2, nxt_chunks)
            while posts:
                posts.pop(0)()
            for ch_ in nxt_chunks:
                ch_()()
        S.barrier()

        R1.clear(); R3.clear()
        acc = R3.alloc([128, NB + 1, D], F32)
        acc_b = [Buf(f"acc{b}") for b in range(NB + 1)]
        for b in range(NB):
            S.load(acc_b[b], acc[:, b, :], xown[b * 128:(b + 1) * 128, :])
        S.load(acc_b[NB], acc[:NS, NB, :], xsmp[:, :])
        NSC = NormScratch(R1, g2)
        wslot = [0]
        for t in range(D // 256):
            wt, wb_ = w_next()
            for b in range(NB + 1):
                cols, P = blk_cols(b)
                s = wslot[0] % 6
                wslot[0] += 1
                bk, half = s // 2, s % 2
                ps = banks[bk][:P, half * 256:(half + 1) * 256]
                mm_group(bk, ps, [(mixT[:, k, cols], wt[:, k, :]) for k in range(16)], reads=(wb_, mixT_b))
                S.op("dve", lambda e, ps=ps, P=P, b=b, t=t: e.tensor_tensor(out=acc[:P, b, t * 256:(t + 1) * 256], in0=ps,
                                                                            in1=acc[:P, b, t * 256:(t + 1) * 256], op=ALU.add),
                     reads=(PB_[bk], acc_b[b]), writes=(acc_b[b],))
        R2.clear()
        h2T = R2.alloc([128, KD, NTS], BF16)
        h2T_b = mixT_b
        items = []
        for b in range(NB + 1):
            cols, P = blk_cols(b)
            items.append(dict(P=P, src_sb=acc[:P, b, :], src_buf=acc_b[b],
                              dst_fn=(lambda k0, k1, cols=cols: h2T[:, k0:k1, cols]), dst_bufs=[h2T_b]))
        norm_seq(NSC, items)
        S.barrier()

        R1.clear()
        hid = [R1.alloc([128, 8, NTS], BF16) for _ in range(2)]
        hid_b = [Buf("hid0"), Buf("hid1")]
        rl = [R3.alloc([128, 512], F32) for _ in range(2)]
        rl_b = [Buf("rl0"), Buf("rl1")]
        tg_all = tgroups + [(NT, NS)]
        urot = [0]
        drot = [0]
        for fg in range(FG):
            hs = fg % 2
            for t in range(4):
                wt, wb_ = w_next()
                for fc in range(2):
                    ch = t * 2 + fc
                    for (t0, tn_) in tg_all:
                        bk = urot[0] % 3
                        ri = urot[0] % 2
                        urot[0] += 1
                        ps = banks[bk][:, 0:tn_]
                        mm_group(bk, ps, [(wt[:, k, fc * 128:(fc + 1) * 128], h2T[:, k, t0:t0 + tn_]) for k in range(KD)],
                                 reads=(wb_, h2T_b))
                        S.op("act", lambda e, ps=ps, ri=ri, tn_=tn_: e.activation(out=rl[ri][:, 0:tn_], in_=ps, func=AF.Relu),
                             reads=(PB_[bk],), writes=(rl_b[ri],))
                        S.op("pool", lambda e, ri=ri, tn_=tn_, hs=hs, ch=ch, t0=t0: e.tensor_tensor(
                            out=hid[hs][:, ch, t0:t0 + tn_], in0=rl[ri][:, 0:tn_], in1=rl[ri][:, 0:tn_], op=ALU.mult),
                            reads=(rl_b[ri],), writes=(hid_b[hs],))
            for t in range(max(1, D // 512)):
                wt, wb_ = w_next()
                for b in range(NB + 1):
                    cols, P = blk_cols(b)
                    bk = 3 + drot[0] % 3
                    drot[0] += 1
                    ps = banks[bk][:P, 0:DN]
                    mm_group(bk, ps, [(hid[hs][:, kk, cols], wt[:, kk, :]) for kk in range(8)], reads=(wb_, hid_b[hs]))
                    S.op("dve", lambda e, ps=ps, P=P, b=b, t=t: e.tensor_tensor(out=acc[:P, b, t * DN:(t + 1) * DN], in0=ps,
                                                                                in1=acc[:P, b, t * DN:(t + 1) * DN], op=ALU.add),
                         reads=(PB_[bk], acc_b[b]), writes=(acc_b[b],))
        for b in range(NB):
            S.store(acc_b[b], y_o[b * 128:(b + 1) * 128, :], acc[:, b, :])
        S.store(acc_b[NB], ys_o[:, :], acc[:NS, NB, :])
        S.finish()

        @block.sync
        def _(e):
            for f in S.prog["sp"]:
                f(e)

        @block.tensor
        def _(e):
            for f in S.prog["pe"]:
                f(e)

        @block.scalar
        def _(e):
            for f in S.prog["act"]:
                f(e)

        @block.vector
        def _(e):
            for f in S.prog["dve"]:
                f(e)

        @block.gpsimd
        def _(e):
            for f in S.prog["pool"]:
                f(e)

    return nc


def _w_in_cols():
    A_Q = 1024
    KVW = 128
    o_ak, o_av = A_Q, A_Q + KVW
    o_rq = A_Q + 2 * KVW
    o_rk = o_rq + 1024
    o_rv = o_rk + 1024
    o_rg = o_rv + 1024
    cols = list(range(0, A_Q)) + list(range(o_ak, o_ak + 128)) + list(range(o_av, o_av + 128))
    cols += list(range(o_ak, o_ak + 128))
    for h in range(RH):
        cols += list(range(o_rq + h * 256, o_rq + (h + 1) * 256))
        cols += list(range(o_rk + h * 256, o_rk + (h + 1) * 256))
        cols += list(range(o_rv + h * 256, o_rv + (h + 1) * 256))
        cols += list(range(o_rg + h * 256, o_rg + (h + 1) * 256))
    return np.asarray(cols)


def make_consts(cfg, core, seq_off):
    NB, NS, PB = cfg["NB"], cfg["NS"], cfg["PB"]
    NT = NB * 128
    gam = gammas()
    half = 128
    inv = (np.float32(ROPE_BASE) ** (-np.arange(half, dtype=np.float32) / np.float32(half))).astype(np.float32)

    def cs(pos):
        ang = pos.astype(np.float32)[None, :] * inv[:, None]
        return np.cos(ang).astype(np.float32), np.sin(ang).astype(np.float32)
    p0 = seq_off
    cT, sT = cs(np.arange(p0, p0 + NT))
    ppos = np.arange(p0 - PB * 128, p0)
    cP, sP = cs(np.maximum(ppos, 0))
    cS, sS = cs(np.full((NS,), PAST_LEN))
    cS, sS = np.ascontiguousarray(cS.T), np.ascontiguousarray(sS.T)
    i = np.arange(128)
    dm = np.zeros((128, RH, 128), np.float32)
    kd = np.zeros((128, RH), np.float32)
    qd = np.zeros((128, RH, 128), np.float32)
    for h in range(RH):
        lg = math.log1p(-2.0 ** (-5.0 - h))
        rel = i[None, :] - i[:, None]
        dm[:, h, :] = np.where(rel >= 0, np.exp(lg * np.maximum(rel, 0)), 0.0) / 16.0
        kd[:, h] = np.exp(lg * (127.0 - i)) / 16.0
        qd[:, h, :] = np.exp(lg * (i + 1.0))[None, :]
    own = (i[:, None] <= i[None, :]).astype(np.float32)
    prv = (i[:, None] >= i[None, :]).astype(np.float32)
    prv_first = prv if core > 0 else np.zeros_like(prv)
    mk = np.stack([np.stack([own, prv_first, own, prv_first], 1), np.stack([own, prv, own, prv], 1)], 1)
    eye = np.eye(NS, dtype=np.float32)
    sel = np.zeros((NS, NS, 128), np.float32)
    for b in range(NS):
        sel[b, b, :] = 1.0
    GP = cfg["GP"]
    kd4 = np.zeros((128, RH, GP), np.float32)
    for h in range(RH):
        lg = math.log1p(-2.0 ** (-5.0 - h))
        for j in range(GP):
            kd4[:, h, j] = np.exp(lg * (127.0 - i + 128.0 * (GP - 1 - j))) / 16.0
    return dict(cosT=cT, sinT=sT, cosP=cP, sinP=sP, cosS=cS, sinS=sS, dmaskT=dm, kdec=kd, qdec=qd, kdec4=kd4,
                masks=mk.astype(ml_dtypes.bfloat16), ident=np.eye(128, dtype=np.float32).astype(ml_dtypes.bfloat16),
                eye=np.ascontiguousarray(np.broadcast_to(eye[None], (128, NS, NS))), eyeT=eye,
                sel=sel.astype(ml_dtypes.bfloat16))


def make_in_maps(cfg, inp):
    NB, NS, PB, NC = cfg["NB"], cfg["NS"], cfg["PB"], cfg["NCORES"]
    NT = NB * 128
    xp = np.asarray(inp["x_prompt"], np.float32)[0]
    xs = np.asarray(inp["x_sample"], np.float32)[:, 0]
    D = xp.shape[1]
    w_in = np.ascontiguousarray(np.asarray(inp["w_in"], np.float32)[0][:, _w_in_cols()])
    shared = dict(
        w_in=w_in,
        w_out=np.ascontiguousarray(np.asarray(inp["w_out"], np.float32)[0]),
        w_up=np.ascontiguousarray(np.asarray(inp["w_up"], np.float32)[0]),
        w_down=np.ascontiguousarray(np.asarray(inp["w_down"], np.float32)[0]),
        g1=np.ascontiguousarray(np.asarray(inp["ln1_g"], np.float32)[0]),
        g2=np.ascontiguousarray(np.asarray(inp["ln2_g"], np.float32)[0]),
        gq=np.ascontiguousarray(np.asarray(inp["q_norm_g"], np.float32)[0]),
        gk=np.ascontiguousarray(np.asarray(inp["k_norm_g"], np.float32)[0]),
        sinks=np.ascontiguousarray(np.asarray(inp["attn_sinks"], np.float32)[0]),
        gret=np.ascontiguousarray(np.asarray(inp["ret_norm_g"], np.float32)[0]),
    )
    ckw = np.asarray(inp["cache_k_win"], np.float32)[0]
    cvw = np.asarray(inp["cache_v_win"], np.float32)[0]
    st = np.asarray(inp["state_ret"], np.float32)[0]
    maps = []
    for c in range(NC):
        p0 = c * NT
        xprev = np.zeros((max(PB, 1) * 128, D), np.float32)
        if PB > 0:
            lo = p0 - PB * 128
            src_lo = max(lo, 0)
            if p0 > src_lo:
                xprev[src_lo - lo:] = xp[src_lo:p0]
        m = dict(shared)
        m.update(make_consts(cfg, c, p0))
        m.update(
            xprev=xprev[:PB * 128] if PB > 0 else xprev,
            xown=np.ascontiguousarray(xp[p0:p0 + NT]),
            xsmp=np.ascontiguousarray(xs[c * NS:(c + 1) * NS]),
            ck=np.ascontiguousarray(ckw[c * NS:(c + 1) * NS].reshape(NS, 128, 128)),
            cv=np.ascontiguousarray(cvw[c * NS:(c + 1) * NS].reshape(NS, 128, 128)),
            st0=np.ascontiguousarray(st[c * NS:(c + 1) * NS]),
        )
        maps.append(m)
    return maps


def assemble(cfg, results):
    NC = cfg["NCORES"]
    y = np.concatenate([r["y"] for r in results], 0)[None]
    ys = np.concatenate([r["ys"] for r in results], 0)[:, None, :]
    last = results[NC - 1]
    kwin = last["kwin"].reshape(1, 1, 128, 2, 64)
    vwin = last["vwin"].reshape(1, 1, 128, 2, 64)
    sret = last["sret"].reshape(1, 1, RH, 256, 256)
    ksw = np.concatenate([r["ksw"] for r in results], 0).reshape(1, -1, 128, 2, 64)
    vsw = np.concatenate([r["vsw"] for r in results], 0).reshape(1, -1, 128, 2, 64)
    ssn = np.concatenate([r["ssn"] for r in results], 0)[None]
    return tuple(np.ascontiguousarray(a, dtype=np.float32) for a in (y, ys, kwin, vwin, sret, ksw, vsw, ssn))


def kernel(**inputs):
    cfg = FULL_CFG
    nc = build_nc(cfg)
    in_maps = make_in_maps(cfg, inputs)
    res = run_bass_kernel_spmd(nc, in_maps, core_ids=list(range(cfg["NCORES"])))
    return assemble(cfg, res.results)
```
